# Optimizing a Trainium2 kernel written in Bass

```python
import jax, jax.numpy as jnp
from jax import lax
import numpy as np

D_MODEL = 1024
BATCH = 4
SEQ = 4096
DEPTH = 2

HEAD_DIM = 64
BLOCK = 128
A_GROUPS = ((128, 1), (512, 4), (2048, 16))
A_HEADS = 12
A_W = A_HEADS * HEAD_DIM
B_Q_HEADS = 16
B_KV_HEADS = 2
B_WINDOW = 128
B_QW = B_Q_HEADS * HEAD_DIM
B_KVW = B_KV_HEADS * HEAD_DIM
C_HEADS = 8
C_KEY_DIM = 64
C_VAL_DIM = 128
C_CHUNK = 128
C_QK = C_HEADS * C_KEY_DIM
C_V = C_HEADS * C_VAL_DIM
N_BRANCH = 3
D_FF = 2816
ROPE_THETA = 10000.0
LN_EPS = 1e-5
ALPHA = (2.0 * DEPTH) ** 0.25
BETA = (8.0 * DEPTH) ** -0.25
IN_SPLITS = (A_W, A_W, A_W, A_W, A_W, A_W, A_W, A_W, A_W, B_QW, B_KVW, B_KVW, C_QK, C_QK, C_V, C_V, N_BRANCH * D_MODEL)
V_COLS = (2, 5, 8, 11, 14)
IN_COLS = sum(IN_SPLITS)

kernel_name = 'hybrid_dilated_swa_retention_macaron_deepnorm'


def layer_norm(x, gain, bias):
    xf = x.astype(jnp.float32)
    mu = jnp.mean(xf, axis=-1, keepdims=True)
    var = jnp.mean(jnp.square(xf - mu), axis=-1, keepdims=True)
    return ((xf - mu) * lax.rsqrt(var + LN_EPS) * gain.astype(jnp.float32) + bias.astype(jnp.float32)).astype(x.dtype)


def rope(x, positions):
    half = x.shape[-1] // 2
    inv = ROPE_THETA ** (-jnp.arange(half, dtype=jnp.float32) / half)
    ang = positions.astype(jnp.float32)[..., None] * inv
    cos = jnp.cos(ang)[:, :, None, :]
    sin = jnp.sin(ang)[:, :, None, :]
    xf = x.astype(jnp.float32)
    x1, x2 = xf[..., :half], xf[..., half:]
    return jnp.concatenate([x1 * cos - x2 * sin, x1 * sin + x2 * cos], axis=-1).astype(x.dtype)


def split_stride(x, d):
    b, s = x.shape[:2]
    rest = x.shape[2:]
    return x.reshape(b, s // d, d, *rest).swapaxes(1, 2).reshape(b * d, s // d, *rest)


def merge_stride(x, d, b):
    l = x.shape[1]
    rest = x.shape[2:]
    return x.reshape(b, d, l, *rest).swapaxes(1, 2).reshape(b, l * d, *rest)


def banded_attention(q, k, v, max_dist, sink=None):
    f32 = jnp.float32
    bsz, L, H, hd = q.shape
    G = k.shape[2]
    rep = H // G
    nb = -(-L // BLOCK)
    pad = nb * BLOCK - L
    padw = ((0, 0), (0, pad), (0, 0), (0, 0))
    qb = jnp.pad(q.astype(f32), padw).reshape(bsz, nb, BLOCK, G, rep, hd)
    kb = jnp.pad(k.astype(f32), padw).reshape(bsz, nb, BLOCK, G, hd)
    vb = jnp.pad(v.astype(f32), padw).reshape(bsz, nb, BLOCK, G, hd)

    def with_prev(t):
        prev = jnp.pad(t, ((0, 0), (1, 0), (0, 0), (0, 0), (0, 0)))[:, :-1]
        return jnp.concatenate([prev, t], axis=2)

    kk, vv = with_prev(kb), with_prev(vb)
    s = jnp.einsum('bnqgrd,bnkgd->bngrqk', qb, kk) * (hd ** -0.5)
    qi = jnp.arange(BLOCK)[:, None] + BLOCK
    ki = jnp.arange(2 * BLOCK)[None, :]
    dist = qi - ki
    band = (dist >= 0) & (dist <= max_dist)
    has_prev = (jnp.arange(nb) > 0)[:, None, None] | (ki >= BLOCK)[None]
    mask = band[None] & has_prev
    s = jnp.where(mask[None, :, None, None], s, -jnp.inf)
    m = jnp.max(s, axis=-1)
    if sink is not None:
        sk = sink.astype(f32).reshape(G, rep)[None, None, :, :, None]
        m = jnp.maximum(m, sk)
    e = jnp.exp(s - m[..., None])
    den = jnp.sum(e, axis=-1)
    if sink is not None:
        den = den + jnp.exp(sk - m)
    den_t = jnp.transpose(den, (0, 1, 4, 2, 3))
    o = jnp.einsum('bngrqk,bnkgd->bnqgrd', e, vv) / den_t[..., None]
    lse = jnp.transpose(m, (0, 1, 4, 2, 3)) + jnp.log(den_t)
    o = o.reshape(bsz, nb * BLOCK, H, hd)[:, :L]
    lse = lse.reshape(bsz, nb * BLOCK, H)[:, :L]
    return o, lse


def retention(q, k, v):
    f32 = jnp.float32
    bsz, S, H, dk = q.shape
    dv = v.shape[-1]
    n = S // C_CHUNK
    log_g = jnp.log1p(-jnp.exp2(-5.0 - jnp.arange(H, dtype=f32)))
    idx = jnp.arange(C_CHUNK, dtype=f32)
    rel = idx[:, None] - idx[None, :]
    intra = jnp.where(rel >= 0, jnp.exp(log_g[:, None, None] * jnp.maximum(rel, 0.0)), 0.0)
    q_dec = jnp.exp(log_g[:, None] * (idx + 1.0))[None, :, :, None]
    k_dec = jnp.exp(log_g[:, None] * (C_CHUNK - 1.0 - idx))[None, :, :, None]
    c_dec = jnp.exp(log_g * C_CHUNK)[None, :, None, None]

    def chunks(t):
        return t.astype(f32).reshape(bsz, n, C_CHUNK, H, t.shape[-1]).transpose(1, 0, 3, 2, 4)

    def step(state, inp):
        qi, ki, vi = inp
        a = jnp.einsum('bhqd,bhkd->bhqk', qi, ki) * intra
        o = jnp.einsum('bhqk,bhkv->bhqv', a, vi) + jnp.einsum('bhqd,bhdv->bhqv', qi, state) * q_dec
        state = state * c_dec + jnp.einsum('bhkd,bhkv->bhdv', ki * k_dec, vi)
        return state, o

    state0 = jnp.zeros((bsz, H, dk, dv), f32)
    _, o = lax.scan(step, state0, (chunks(q), chunks(k), chunks(v)))
    return o.transpose(1, 0, 3, 2, 4).reshape(bsz, S, H, dv)


def swiglu(x, w_up, w_down):
    a, b = jnp.split(x @ w_up, 2, axis=-1)
    return (jax.nn.silu(a) * b) @ w_down


def hybrid_mixer(x, positions, w_in, gate_bias, sinks, w_proj_a, w_proj_b, w_proj_c, w_out):
    bsz, S, _ = x.shape
    points, acc = [], 0
    for size in IN_SPLITS[:-1]:
        acc += size
        points.append(acc)
    parts = jnp.split(x @ w_in, points, axis=-1)

    def heads(t, nh):
        return t.reshape(bsz, S, nh, -1)

    outs, lses = [], []
    for gi, (window, dil) in enumerate(A_GROUPS):
        q = rope(heads(parts[3 * gi], A_HEADS), positions)
        k = rope(heads(parts[3 * gi + 1], A_HEADS), positions)
        v = heads(parts[3 * gi + 2], A_HEADS)
        o, lse = banded_attention(split_stride(q, dil), split_stride(k, dil), split_stride(v, dil), window // dil)
        outs.append(merge_stride(o, dil, bsz))
        lses.append(merge_stride(lse, dil, bsz))
    wts = jax.nn.softmax(jnp.stack(lses, axis=0), axis=0)[..., None]
    y_a = jnp.sum(wts * jnp.stack(outs, axis=0), axis=0).reshape(bsz, S, A_W).astype(x.dtype)

    qb = rope(heads(parts[9], B_Q_HEADS), positions)
    kb = rope(heads(parts[10], B_KV_HEADS), positions)
    vb = heads(parts[11], B_KV_HEADS)
    ob, _ = banded_attention(qb, kb, vb, B_WINDOW - 1, sinks)
    y_b = ob.reshape(bsz, S, B_QW).astype(x.dtype)

    qc = rope(heads(parts[12], C_HEADS), positions)
    kc = rope(heads(parts[13], C_HEADS), positions) * (C_KEY_DIM ** -0.5)
    vc = heads(parts[14], C_HEADS)
    r = retention(qc, kc, vc)
    mu = jnp.mean(r, axis=-1, keepdims=True)
    var = jnp.mean(jnp.square(r - mu), axis=-1, keepdims=True)
    r = ((r - mu) * lax.rsqrt(var + LN_EPS)).reshape(bsz, S, C_V)
    y_c = (jax.nn.silu(parts[15].astype(jnp.float32)) * r).astype(x.dtype)

    g_a, g_b, g_c = jnp.split(jax.nn.sigmoid(parts[16] + gate_bias), N_BRANCH, axis=-1)
    merged = g_a * (y_a @ w_proj_a) + g_b * (y_b @ w_proj_b) + g_c * (y_c @ w_proj_c)
    return merged @ w_out


def setup_inputs(seed: int = 0) -> dict:
    key = jax.random.key(seed)
    ks = jax.random.split(key, 20)
    f32 = jnp.float32

    def nrm(k, shape, scale):
        return jax.random.normal(k, shape, f32) * scale

    x = jax.random.normal(ks[0], (BATCH, SEQ, D_MODEL), f32)
    offsets = jax.random.randint(ks[1], (BATCH, 1), 0, 4096, dtype=jnp.int32)
    positions = (jnp.arange(SEQ, dtype=jnp.int32)[None, :] + offsets).astype(jnp.int32)
    col_scale = jnp.concatenate([jnp.full((n,), BETA if i in V_COLS else 1.0, f32) for i, n in enumerate(IN_SPLITS)])
    w_in = nrm(ks[2], (DEPTH, D_MODEL, IN_COLS), D_MODEL ** -0.5) * col_scale
    gate_bias = nrm(ks[3], (DEPTH, N_BRANCH * D_MODEL), 0.02)
    attn_sinks = nrm(ks[4], (DEPTH, B_Q_HEADS), 0.5)
    w_proj_a = nrm(ks[5], (DEPTH, A_W, D_MODEL), BETA * A_W ** -0.5)
    w_proj_b = nrm(ks[6], (DEPTH, B_QW, D_MODEL), BETA * B_QW ** -0.5)
    w_proj_c = nrm(ks[7], (DEPTH, C_V, D_MODEL), BETA * C_V ** -0.5)
    w_out = nrm(ks[8], (DEPTH, D_MODEL, D_MODEL), BETA * D_MODEL ** -0.5)
    ffn1_up = nrm(ks[9], (DEPTH, D_MODEL, 2 * D_FF), D_MODEL ** -0.5)
    ffn1_down = nrm(ks[10], (DEPTH, D_FF, D_MODEL), BETA * D_FF ** -0.5)
    ffn2_up = nrm(ks[11], (DEPTH, D_MODEL, 2 * D_FF), D_MODEL ** -0.5)
    ffn2_down = nrm(ks[12], (DEPTH, D_FF, D_MODEL), BETA * D_FF ** -0.5)
    ln1_g = 1.0 + nrm(ks[13], (DEPTH, D_MODEL), 0.02)
    ln1_b = nrm(ks[14], (DEPTH, D_MODEL), 0.02)
    ln2_g = 1.0 + nrm(ks[15], (DEPTH, D_MODEL), 0.02)
    ln2_b = nrm(ks[16], (DEPTH, D_MODEL), 0.02)
    ln3_g = 1.0 + nrm(ks[17], (DEPTH, D_MODEL), 0.02)
    ln3_b = nrm(ks[18], (DEPTH, D_MODEL), 0.02)
    return {'x': x, 'positions': positions, 'w_in': w_in, 'gate_bias': gate_bias, 'attn_sinks': attn_sinks,
            'w_proj_a': w_proj_a, 'w_proj_b': w_proj_b, 'w_proj_c': w_proj_c, 'w_out': w_out,
            'ffn1_up': ffn1_up, 'ffn1_down': ffn1_down, 'ffn2_up': ffn2_up, 'ffn2_down': ffn2_down,
            'ln1_g': ln1_g, 'ln1_b': ln1_b, 'ln2_g': ln2_g, 'ln2_b': ln2_b, 'ln3_g': ln3_g, 'ln3_b': ln3_b}


def reference(x, positions, w_in, gate_bias, attn_sinks, w_proj_a, w_proj_b, w_proj_c, w_out,
              ffn1_up, ffn1_down, ffn2_up, ffn2_down, ln1_g, ln1_b, ln2_g, ln2_b, ln3_g, ln3_b):
    for l in range(DEPTH):
        x = layer_norm(ALPHA * x + 0.5 * swiglu(x, ffn1_up[l], ffn1_down[l]), ln1_g[l], ln1_b[l])
        mix = hybrid_mixer(x, positions, w_in[l], gate_bias[l], attn_sinks[l],
                           w_proj_a[l], w_proj_b[l], w_proj_c[l], w_out[l])
        x = layer_norm(ALPHA * x + mix, ln2_g[l], ln2_b[l])
        x = layer_norm(ALPHA * x + 0.5 * swiglu(x, ffn2_up[l], ffn2_down[l]), ln3_g[l], ln3_b[l])
    return x
```

```python
import numpy as np
import ml_dtypes
from contextlib import ExitStack
import concourse.bass as bass
import concourse.mybir as mybir
from concourse.bass_utils import run_bass_kernel_spmd

F32 = mybir.dt.float32
BF16 = mybir.dt.bfloat16
I32 = mybir.dt.int32
ALU = mybir.AluOpType
AF = mybir.ActivationFunctionType

D = 1024
KD = 8
NT = 2048
SP = 512
NSP = NT // SP
DFF = 2816
NFC = DFF // 128
DEPTH = 2
ALPHA = (2.0 * DEPTH) ** 0.25
LN_EPS = 1e-5
INCOLS = 14336
A_W = 768
OFF_A = [(3 * g * A_W, (3 * g + 1) * A_W, (3 * g + 2) * A_W) for g in range(3)]
OFF_BQ = 9 * A_W
OFF_BK = OFF_BQ + 1024
OFF_BV = OFF_BK + 128
OFF_CQ = OFF_BV + 128
OFF_CK = OFF_CQ + 512
OFF_CV = OFF_CK + 512
OFF_CG = OFF_CV + 1024
OFF_G = OFF_CG + 1024
DIL = (1, 4, 16)
HL = (128, 512, 2048)
HLB = 128


class Res:
    __slots__ = ("name", "last_w", "readers", "excl")

    def __init__(self, name="", excl=False):
        self.name = name
        self.last_w = None
        self.readers = []
        self.excl = excl


class Sched:
    ENGS = ("pe", "act", "dve", "pool", "sp")

    def __init__(self, nc, stack, n_dma_sems=40):
        self.nc = nc
        self.ops = {e: [] for e in self.ENGS}
        self.count = {}
        self.sem = {}
        self.seen = {e: {} for e in self.ENGS}
        for e in self.ENGS:
            self.sem[e] = stack.enter_context(nc.semaphore("s_" + e))
            self.count[e] = 0
        self.dma_pool = []
        self.qpool = {"sp": [], "pool": [], "act": []}
        for i in range(n_dma_sems):
            k = "dma%d" % i
            self.sem[k] = stack.enter_context(nc.semaphore("s_" + k))
            self.count[k] = 0
            self.dma_pool.append(k)
            self.qpool["sp" if i < 24 else ("pool" if i < 36 else "act")].append(k)
        self.dma_rr = {"sp": 0, "pool": 0, "act": 0}
        self.cc_keys = []
        self.cc_i = 0
        for i in range(20):
            k = "cc%d" % i
            self.sem[k] = stack.enter_context(nc.semaphore("s_" + k))
            self.count[k] = 0
            self.cc_keys.append(k)
        self.out_toks = []

    def _deps(self, eng, reads, writes):
        need = {}

        def add(tok):
            if tok is None:
                return
            k, v = tok
            if need.get(k, 0) < v:
                need[k] = v
        for r in reads:
            add(r.last_w)
            if r.excl:
                for rd in r.readers:
                    add(rd)
        for w in writes:
            add(w.last_w)
            for rd in w.readers:
                add(rd)
        waits = []
        for k, v in need.items():
            if k == eng and eng == "pe":
                continue
            if self.seen[eng].get(k, 0) >= v:
                continue
            self.seen[eng][k] = v
            waits.append((k, v))
        return waits

    def _commit(self, tok, reads, writes):
        for r in reads:
            r.readers.append(tok)
            if len(r.readers) > 64:
                best = {}
                for k, v in r.readers:
                    if best.get(k, 0) < v:
                        best[k] = v
                r.readers = list(best.items())
        for w in writes:
            w.last_w = tok
            w.readers = []

    def op(self, eng, fn, reads=(), writes=(), noinc=False):
        waits = self._deps(eng, reads, writes)
        if noinc:
            assert eng == "pe"
            tok = (eng, self.count[eng] + 1)
            self.ops[eng].append((waits, fn, eng, 0))
        else:
            self.count[eng] += 1
            tok = (eng, self.count[eng])
            self.ops[eng].append((waits, fn, eng, 1))
        self._commit(tok, reads, writes)
        return tok

    def I(self, eng, name, reads, writes, *args, **kw):
        noinc = kw.pop("_noinc", False)
        return self.op(eng, lambda e: getattr(e, name)(*args, **kw), reads, writes, noinc=noinc)

    def D(self, eng, out, in_, reads=(), writes=(), is_output=False):
        return self.dma(eng, lambda e: e.dma_start(out=out, in_=in_), reads, writes, is_output)

    def dma(self, eng, fn, reads=(), writes=(), is_output=False):
        waits = self._deps(eng, reads, writes)
        pool = self.qpool[eng]
        key = pool[self.dma_rr[eng] % len(pool)]
        self.dma_rr[eng] += 1
        if self.count[key] > 0 and self.seen[eng].get(key, 0) < self.count[key]:
            self.seen[eng][key] = self.count[key]
            waits.append((key, self.count[key]))
        self.count[key] += 16
        tok = (key, self.count[key])
        self.ops[eng].append((waits, fn, key, 16))
        self._commit(tok, reads, writes)
        if is_output:
            self.out_toks.append(tok)
        return tok

    def coll(self, fn, reads=(), writes=()):
        waits = self._deps("pool", reads, writes)
        key = self.cc_keys[self.cc_i]
        self.cc_i += 1
        self.count[key] += 1
        tok = (key, self.count[key])
        self.ops["pool"].append((waits, fn, key, 1))
        self._commit(tok, reads, writes)
        return tok

    def inherit(self, new_list, old_list):
        toks = []
        for o in old_list:
            if o.last_w is not None:
                toks.append(o.last_w)
            toks.extend(o.readers)
        best = {}
        for k, v in toks:
            if best.get(k, 0) < v:
                best[k] = v
        toks = list(best.items())
        for n in new_list:
            n.readers = list(n.readers) + toks

    def barrier(self):
        snap = [(k, v) for k, v in self.count.items() if v > 0]
        for e in self.ENGS:
            waits = []
            for k, v in snap:
                if k == e:
                    continue
                if self.seen[e].get(k, 0) >= v:
                    continue
                self.seen[e][k] = v
                waits.append((k, v))
            if waits:
                self.ops[e].append((waits, None, None, 0))

    def barrier_engines(self):
        snap = [(k, v) for k, v in self.count.items() if v > 0 and not k.startswith("cc")]
        for e in self.ENGS:
            waits = []
            for k, v in snap:
                if k == e or self.seen[e].get(k, 0) >= v:
                    continue
                self.seen[e][k] = v
                waits.append((k, v))
            if waits:
                self.ops[e].append((waits, None, None, 0))

    def finish(self, eng="sp"):
        waits = [(k, self.count[k]) for k in self.dma_pool if self.count[k] > 0]
        self.ops[eng].append((waits, None, None, 0))

    def run(self, block):
        engmap = {"pe": "tensor", "act": "scalar", "dve": "vector", "pool": "gpsimd", "sp": "sync"}
        for e in self.ENGS:
            ops = self.ops[e]
            if not ops:
                continue

            def body(engine, ops=ops):
                for waits, fn, key, amt in ops:
                    for k, v in waits:
                        engine.wait_ge(self.sem[k], v)
                    if fn is not None:
                        ins = fn(engine)
                        if amt:
                            ins.then_inc(self.sem[key], amt)
            getattr(block, engmap[e])(body)


class Ring:
    def __init__(self, items):
        self.items = items
        self.i = 0

    def next(self):
        it = self.items[self.i % len(self.items)]
        self.i += 1
        return it


def _consts():
    c = {}
    c["ident"] = np.eye(128, dtype=np.float32)
    rot = np.zeros((128, 128), np.float32)
    for m in range(128):
        if m % 64 < 32:
            rot[m + 32, m] = -1.0
        else:
            rot[m - 32, m] = 1.0
    c["rot"] = rot
    c["ones_ln"] = np.full((128, 128), 1.0 / 1024.0, np.float32)
    c["ones_gn"] = np.full((128, 128), 1.0 / 128.0, np.float32)
    c["ones"] = np.ones((128, 128), np.float32)
    k = np.arange(128)[:, None]
    q = np.arange(128)[None, :]
    NEG = -30000.0
    cur = np.where(k <= q, 0.0, NEG).astype(np.float32)
    prevA = np.where(k >= q, 0.0, NEG).astype(np.float32)
    prevB = np.where(k > q, 0.0, NEG).astype(np.float32)
    c["maskA"] = np.concatenate([prevA, cur], 1)
    c["maskB"] = np.concatenate([prevB, cur], 1)
    z = np.full_like(cur, NEG)
    c["maskZ"] = np.concatenate([z, cur], 1)
    for n in ("maskA", "maskB", "maskZ"):
        m = (c[n] == 0.0).astype(np.float32)
        c["m" + n] = np.concatenate([m, m], 1)
    return c


BF_NAMES = ["ident", "rot", "ones_ln", "ones_gn", "ones", "maskA", "maskB", "maskA0", "maskB0",
            "mmaskA", "mmaskB", "mmaskA0", "mmaskB0"]
BF_W = {"ident": 128, "rot": 128, "ones_ln": 128, "ones_gn": 128, "ones": 128,
        "maskA": 256, "maskB": 256, "maskA0": 256, "maskB0": 256,
        "mmaskA": 512, "mmaskB": 512, "mmaskA0": 512, "mmaskB0": 512}


def _bf_pack(is_odd):
    c = _consts()
    c["maskA0"] = c["maskA"] if is_odd else c["maskZ"]
    c["maskB0"] = c["maskB"] if is_odd else c["maskZ"]
    c["mmaskA0"] = c["mmaskA"] if is_odd else c["mmaskZ"]
    c["mmaskB0"] = c["mmaskB"] if is_odd else c["mmaskZ"]
    return np.concatenate([c[n] for n in BF_NAMES], 1).astype(ml_dtypes.bfloat16)


def _f32_consts(is_odd):
    parts = {}
    half = 32
    inv = (10000.0 ** (-np.arange(half, dtype=np.float32) / half)).astype(np.float32)
    p = np.arange(128)
    parts["invf"] = (inv[p % 32].astype(np.float64) / (2 * np.pi)).astype(np.float32)[:, None]
    parts["hv"] = np.full((128, 1), 1.0 if is_odd else 0.0, np.float32)
    parts["eps_ln"] = np.full((128, 1), LN_EPS / ALPHA ** 2, np.float32)
    parts["eps_gn"] = np.full((128, 1), LN_EPS, np.float32)
    gam = 1.0 - 2.0 ** (-5.0 - np.arange(8, dtype=np.float64))
    lg = np.log(gam)
    intra = np.zeros((128, 4, 256), np.float64)
    qdec = np.zeros((128, 4, 128), np.float64)
    kdec = np.zeros((128, 4, 128), np.float64)
    cdec = np.zeros((128, 4), np.float64)
    kk = np.arange(128)[:, None]
    qq = np.arange(128)[None, :]
    for j in range(4):
        for hh in range(2):
            h = 2 * j + hh
            rel = qq - kk
            intra[:, j, hh * 128:(hh + 1) * 128] = np.where(rel >= 0, np.exp(lg[h] * np.maximum(rel, 0)), 0.0) * 0.125
            qdec[hh * 64:(hh + 1) * 64, j, :] = np.exp(lg[h] * (np.arange(128) + 1.0))[None, :]
            kdec[:, j, hh * 64:(hh + 1) * 64] = (np.exp(lg[h] * (127.0 - np.arange(128))) * 0.125)[:, None]
            cdec[hh * 64:(hh + 1) * 64, j] = np.exp(lg[h] * 128.0)
    shdn = np.zeros((128, 128), np.float32)
    shup = np.zeros((128, 128), np.float32)
    for m in range(64):
        shdn[m + 64, m] = 1.0
        shup[m, m + 64] = 1.0
    parts["shdn"] = shdn
    parts["shup"] = shup
    parts["intra"] = intra.reshape(128, -1)
    parts["qdec"] = qdec.reshape(128, -1)
    parts["kdec"] = kdec.reshape(128, -1)
    parts["cdec"] = cdec
    offs = {}
    o = 0
    arrs = []
    for n, a in parts.items():
        a = np.asarray(a, np.float32)
        offs[n] = (o, a.shape[1])
        o += a.shape[1]
        arrs.append(a)
    return np.concatenate(arrs, 1), offs


F32C, F32_OFFS = _f32_consts(True)
NF32C = F32C.shape[1]


HK_T = [(128, 7168), (128, 7680), (128, 1536)]
HV_T = [(1024, A_W), (1024, A_W), (640, A_W)]


def kloc(g, j):
    if g == 2:
        return (0, j * 2048) if j < 3 else (1, (j - 3) * 2048)
    if g == 0:
        return (0, 6144 + j * 128)
    return (1, 6144 + j * 512) if j < 3 else (2, (j - 3) * 512)


def kloc_b(kv):
    return (0, 6144 + 768 + kv * 128)


def vloc(g, row):
    if g == 2:
        return (0, row) if row < 1024 else (1, row - 1024)
    if g == 0:
        return (2, row)
    return (2, 128 + row)


MASK_ENG = "pool"
DESTRIDE = False


class Prog:
    def __init__(self, stage, dbg=None):
        self.stage = stage
        self.dbg = dbg
        self.nc = bass.Bass("TRN2", target_bir_lowering=False)
        self.st = ExitStack()

    def sb(self, name, shape, dt):
        return self.st.enter_context(self.nc.sbuf_tensor(name, shape, dt))

    def din(self, name, shape, dt):
        return self.nc.dram_tensor(name, shape, dt, kind="ExternalInput").ap()

    def dout(self, name, shape, dt):
        return self.nc.dram_tensor(name, shape, dt, kind="ExternalOutput").ap()

    def dint(self, name, shape, dt):
        return self.nc.dram_tensor(name, shape, dt, kind="Internal").ap()

    def build(self):
        nc = self.nc
        st = self.st
        with st:
            self._declare()
            self.S = Sched(nc, st)
            block = st.enter_context(nc.Block())
            self._program()
            self.S.finish("sp")
            self.S.run(block)
        return nc

    def _declare(self):
        nc = self.nc
        L = DEPTH
        self.d_xT = self.din("xT", [D, NT], F32)
        self.d_pos = self.din("pos", [1, NT], I32)
        self.d_w_in = self.din("w_in", [L, D, INCOLS], F32)
        self.d_wpa = self.din("w_proj_a", [L, A_W, D], F32)
        self.d_wpb = self.din("w_proj_b", [L, D, D], F32)
        self.d_wpc = self.din("w_proj_c", [L, D, D], F32)
        self.d_wout = self.din("w_out", [L, D, D], F32)
        self.d_f1u = self.din("ffn1_up", [L, D, 2 * DFF], F32)
        self.d_f1d = self.din("ffn1_down", [L, DFF, D], F32)
        self.d_f2u = self.din("ffn2_up", [L, D, 2 * DFF], F32)
        self.d_f2d = self.din("ffn2_down", [L, DFF, D], F32)
        self.d_lnp = self.din("lnp", [128, L * 3 * 2 * 8], F32)
        self.d_gb = self.din("gbias", [128, L * 24], F32)
        self.d_sk = self.din("sinks", [128, L * 8], F32)
        nbf = sum(BF_W[n] for n in BF_NAMES)
        self.d_cbf = self.din("cbf", [128, nbf], BF16)
        self.d_cf32 = self.din("cf32", [128, NF32C], F32)
        self.hin = {}
        self.hout = {}
        self.cc_pairs = {}
        for l in range(DEPTH):
            pairs = []

            def mk(name, rows, cols, dt):
                o = self.dint("%s_o%d" % (name, l), [rows, cols], dt)
                g = self.dint("%s_g%d" % (name, l), [2 * rows, cols], dt)
                pairs.append((o, g))
                return o, g[0:rows, :]
            ks = [mk("hk%d" % i, r, c, BF16) for i, (r, c) in enumerate(HK_T)]
            vs = [mk("hv%d" % i, r, c, BF16) for i, (r, c) in enumerate(HV_T)]
            vb = mk("hvb", HLB, 128, BF16)
            ss = mk("hs", 512, 128, F32)
            self.hout[l] = dict(k=[k[0] for k in ks], va=[v[0] for v in vs], vb=vb[0],
                                s=ss[0].rearrange("(j p) c -> j p c", p=128))
            self.hin[l] = dict(k=[k[1] for k in ks], va=[v[1] for v in vs], vb=vb[1],
                               s=ss[1].rearrange("(j p) c -> j p c", p=128))
            self.cc_pairs[l] = pairs
            if not hasattr(self, "R_hout_l"):
                self.R_hout_l = {}
            self.R_hout_l[l] = [Res() for _ in pairs]
        self.R_hin = Res()
        self.d_out = self.dout("outT", [D, NT], F32)
        self.kTd_A = [self.dint("kTdA%d" % g, [6, 128, HL[g] + NT], BF16) for g in range(3)]
        self.kTd_B = self.dint("kTdB", [2, 128, HLB + NT], BF16)
        self.kTd_C = self.dint("kTdC", [4, 128, NT], BF16)
        self.Vd_A = [self.dint("VdA%d" % g, [HL[g] + NT, A_W], BF16) for g in range(3)]
        self.Vd_B = self.dint("VdB", [HLB + NT, 128], BF16)
        self.Vd_C = self.dint("VdC", [NT, 1024], BF16)
        self.yT_d = self.dint("yTd", [22, 128, NT], BF16)
        self.xsp = self.dint("xsp", [128, KD, NT], F32)
        self.mixd = self.dint("mixd", [KD, 128, NT], F32)
        self.R_mixd = [Res() for s in range(NSP)]
        self.R_kTdA = [[Res() for j in range(6)] for g in range(3)]
        self.R_kTdB = [Res(), Res()]
        self.R_kTdC = [Res() for j in range(4)]
        self.R_VdA = [Res() for g in range(3)]
        self.R_VdB = Res()
        self.R_VdC = [Res() for j in range(4)]
        self.R_yTd = [Res() for c in range(22)]
        self.R_xsp = [Res() for s in range(NSP)]
        self.R_hout = Res()
        self.XR = self.sb("XR", [128, KD * NT], F32)
        self.xres = self.XR[:].rearrange("p (c t) -> p c t", c=KD)
        self.XRb = self.XR[:].bitcast(BF16)
        self.xbf = self.sb("xbf", [128, KD, NT], BF16)
        self.R_xres = [[Res("xres%d_%d" % (c, s)) for s in range(NSP)] for c in range(KD)]
        self.R_xbf = [[Res("xbf%d_%d" % (c, s)) for s in range(NSP)] for c in range(KD)]
        self.cbf = self.sb("cbf_sb", [128, nbf], BF16)
        self.cf32 = self.sb("cf32_sb", [128, NF32C], F32)
        self.lnp = self.sb("lnp_sb", [128, L * 3 * 2 * 8], F32)
        self.gb = self.sb("gb_sb", [128, L * 24], F32)
        self.sk = self.sb("sk_sb", [128, L * 8], F32)
        self.esk = self.sb("esk_sb", [128, 8], F32)
        self.R_esk = Res()
        self.R_const = Res("const")
        self.wsmall = Ring([(self.sb("wsm%d" % i, [128, 2048], BF16), Res()) for i in range(6)])
        self.wbig = Ring([(self.XRb[:, 16384 + i * 8192:16384 + (i + 1) * 8192], Res()) for i in range(2)])
        self.wmid = Ring([(self.XRb[:, 16384 + i * 4096:16384 + (i + 1) * 4096], Res()) for i in range(4)])
        self.psum = Ring([(self.st.enter_context(nc.psum_tensor("ps%d" % i, [128, 512], F32)), Res("ps%d" % i, excl=True))
                          for i in range(8)])
        self.fwork = self.sb("fwork", [128, 8192], BF16)
        self.hT = self.fwork[:].rearrange("p (c t) -> p c t", c=2)
        self.R_hT = [[Res() for s in range(NSP)] for c in range(2)]
        self.zbf = self.fwork[:, 0:4096].rearrange("p (c t) -> p c t", c=KD)
        self.zsq = self.fwork[:, 4096:8192].rearrange("p (c t) -> p c t", c=KD)
        self.R_zbf = Res("zbf")
        self.R_zsq = Res("zsq")
        self.kq = Ring([(self.fwork[:, i * 2048:(i + 1) * 2048], [Res() for s in range(NSP)]) for i in range(2)])
        self.kq2 = Ring([(self.fwork[:, 4096 + i * 2048:4096 + (i + 1) * 2048], [Res() for s in range(NSP)])
                         for i in range(2)])
        self.kq4 = Ring(self.kq.items + self.kq2.items)
        self.ysp = Ring([(self.fwork[:, i * 4096:(i + 1) * 4096], Res()) for i in range(2)])
        self.kx = self.XRb[:, 0:4096]
        self.R_kx = Res()
        self.vx = self.XRb[:, 4096:10240]
        self.R_vx = Res()
        self.R_vCt = [Res() for t in range(16)]
        self.kvr = Ring([((self.kx, self.R_kx), (self.vx, self.R_vx)),
                         ((self.XRb[:, 10240:14336], Res()), (self.XRb[:, 14336:20480], Res()))])
        self.acc = self.XR[:, 10240:14336].rearrange("p (n t) -> p n t", n=2)
        self.R_accf = [[Res() for rho in range(16)] for b in range(16)]
        self.R_acc = [[self.R_accf[b][rho] for b in range(4 * s, 4 * s + 4) for rho in range(16)] for s in range(NSP)]
        self.merged = self.XRb[:, 0:16384].rearrange("p (c t) -> p c t", c=KD)
        self.R_merged = [[Res() for s in range(NSP)] for c in range(KD)]
        self.tmpf = Ring([(self.sb("tmpf%d" % i, [128, 512], F32), Res()) for i in range(8)])
        self.tmpb = Ring([(self.sb("tmpb%d" % i, [128, 512], BF16), Res()) for i in range(8)])
        self.tmpb32 = Ring([(self.sb("lnt%d" % i, [128, SP], F32), Res()) for i in range(2)])
        self.R_lnmean = Res("lnmean")
        self.R_lnrstd = Res("lnrstd")
        self.cosT = self.sb("cosT", [128, NT], F32)
        self.sinT = self.sb("sinT", [128, NT], F32)
        self.R_rope = [Res() for s in range(NSP)]
        self.posi = self.sb("posi", [128, SP], I32)
        self.R_posi = Res()
        self.Sf = self.sb("Sf", [128, 128], F32)
        self.Sb = self.sb("Sb", [128, 128], BF16)
        self.R_Sf = Res()
        self.R_Sb = Res()

    def fwork_res(self, exclude=()):
        out = []
        for row in self.R_hT:
            out += row
        out += [self.R_zbf, self.R_zsq]
        for ring in (self.kq, self.kq2):
            for (_, rl) in ring.items:
                out += rl
        for (_, r) in self.ysp.items:
            out.append(r)
        ex = set(id(x) for x in exclude)
        return [r for r in out if id(r) not in ex]

    def arena_res(self):
        out = [r for (_, r) in self.wbig.items] + [r for (_, r) in self.wmid.items]
        for (kb, vb) in self.kvr.items:
            out += [kb[1], vb[1]]
        out += self.R_vCt
        for row in self.R_accf:
            out += row
        for row in self.R_merged:
            out += row
        return out

    def cb(self, name):
        o = 0
        for n in BF_NAMES:
            if n == name:
                return self.cbf[:, o:o + BF_W[n]]
            o += BF_W[n]
        raise KeyError(name)

    def cf(self, name):
        o, w = F32_OFFS[name]
        return self.cf32[:, o:o + w]

    def _program(self):
        S = self.S
        for dst, src in ((self.cbf, self.d_cbf), (self.cf32, self.d_cf32), (self.lnp, self.d_lnp),
                         (self.gb, self.d_gb), (self.sk, self.d_sk)):
            S.D("sp", dst[:], src, writes=[self.R_const])
        xv = self.d_xT.rearrange("(c p) t -> p c t", p=128)
        for s in range(NSP):
            sl = slice(s * SP, (s + 1) * SP)
            S.D("sp", self.xres[:, :, sl], xv[:, :, sl], writes=[self.R_xres[c][s] for c in range(KD)])
            for c in range(KD):
                S.I("act", "copy", [self.R_xres[c][s]], [self.R_xbf[c][s]], out=self.xbf[:, c, sl], in_=self.xres[:, c, sl])
        self.rope_tables()
        for l in range(DEPTH):
            self.ffn(l, self.d_f1u, self.d_f1d)
            if self.dbg == "ffn1":
                return self.store_x()
            self.layernorm(l, 0)
            if self.dbg == "ln1":
                return self.store_x()
            for s in range(NSP):
                sl = slice(s * SP, (s + 1) * SP)
                S.D("sp", self.xsp[:, :, sl], self.xres[:, :, sl], reads=[self.R_xres[c][s] for c in range(KD)],
                    writes=[self.R_xsp[s]])
            S.inherit(self.arena_res(), [r for row in self.R_xres for r in row])
            self.m1(l)
            self.exchange(l)
            S.barrier_engines()
            self.m2a(l)
            S.barrier()
            self.m2b(l)
            self.layernorm(l, 1)
            if self.dbg == "ln2":
                return self.store_x()
            self.ffn(l, self.d_f2u, self.d_f2d)
            self.layernorm(l, 2)
        self.store_x()

    def store_x(self):
        ov = self.d_out.rearrange("(c p) t -> p c t", p=128)
        for c in range(KD):
            self.S.D("sp", ov[:, c, :], self.xres[:, c, :], reads=self.R_xres[c], is_output=True)

    def load_w(self, dram_ap, ncols, nk, big=False, mid=False):
        if mid:
            wt, wr = self.wmid.next()
            assert nk * ncols <= 4096
            base = wt
        elif big:
            wt, wr = self.wbig.next()
            assert nk * ncols <= 8192
            base = wt
        else:
            wt, wr = self.wsmall.next()
            assert nk * ncols <= 2048
            base = wt[:]
        view = base[:, 0:nk * ncols].rearrange("p (k f) -> p k f", k=nk)
        src = dram_ap.rearrange("(k p) f -> p k f", p=128)
        self.S.D("pool", view, src, writes=[wr])
        return view, wr

    def rope_tables(self):
        S = self.S
        RC = self.R_const
        invf = self.cf("invf")
        two_pi = float(2.0 * np.pi)
        for s in range(NSP):
            sl = slice(s * SP, (s + 1) * SP)
            S.D("sp", self.posi[:], self.d_pos[:, sl].partition_broadcast(128), writes=[self.R_posi])
            t0, r0 = self.tmpf.next()
            t1, r1 = self.tmpf.next()
            t2, r2 = self.tmpf.next()
            S.I("dve", "tensor_copy", [self.R_posi], [r0], out=t0[:], in_=self.posi[:])
            S.I("dve", "tensor_scalar", [r0, RC], [r0], out=t0[:], in0=t0[:], scalar1=invf, scalar2=None, op0=ALU.mult)
            S.I("dve", "tensor_copy", [r0], [self.R_posi], out=self.posi[:], in_=t0[:])
            S.I("dve", "tensor_copy", [self.R_posi], [r1], out=t1[:], in_=self.posi[:])
            S.I("dve", "tensor_tensor", [r0, r1], [r0], out=t0[:], in0=t0[:], in1=t1[:], op=ALU.subtract)
            for (shift, dst) in ((0.0, self.sinT), (0.25, self.cosT)):
                S.I("dve", "tensor_scalar", [r0], [r2], out=t2[:], in0=t0[:], scalar1=float(shift), scalar2=None, op0=ALU.add)
                S.I("dve", "tensor_scalar", [r2], [r1], out=t1[:], in0=t2[:], scalar1=0.5, scalar2=-1.0,
                    op0=ALU.is_gt, op1=ALU.mult)
                S.I("dve", "tensor_tensor", [r2, r1], [r2], out=t2[:], in0=t2[:], in1=t1[:], op=ALU.add)
                S.I("dve", "tensor_scalar", [r2], [r1], out=t1[:], in0=t2[:], scalar1=-0.5, scalar2=None, op0=ALU.is_lt)
                S.I("dve", "tensor_tensor", [r2, r1], [r2], out=t2[:], in0=t2[:], in1=t1[:], op=ALU.add)
                S.I("act", "activation", [r2], [self.R_rope[s]], out=dst[:, sl], in_=t2[:], func=AF.Sin, scale=two_pi)

    def ffn(self, l, d_up, d_down):
        S = self.S
        cffn = 0.5 / ALPHA
        FG = 2
        groups = [(g * FG, min(FG, NFC - g * FG)) for g in range((NFC + FG - 1) // FG)]
        for (f0, nf) in groups:
            wa, ra = self.load_w(d_up[l, :, f0 * 128:(f0 + nf) * 128], nf * 128, KD)
            wb, rb = self.load_w(d_up[l, :, DFF + f0 * 128:DFF + (f0 + nf) * 128], nf * 128, KD)
            wd, rd = self.load_w(d_down[l, f0 * 128:(f0 + nf) * 128, :], D, nf)
            for ci in range(nf):
                for s in range(NSP):
                    sl = slice(s * SP, (s + 1) * SP)
                    pa, rpa = self.psum.next()
                    pb, rpb = self.psum.next()
                    for k in range(KD):
                        S.I("pe", "matmul", [ra, self.R_xbf[k][s]], [rpa], pa[:], lhsT=wa[:, k, ci * 128:(ci + 1) * 128],
                            rhs=self.xbf[:, k, sl], start=(k == 0), stop=(k == KD - 1), _noinc=(k != KD - 1))
                    for k in range(KD):
                        S.I("pe", "matmul", [rb, self.R_xbf[k][s]], [rpb], pb[:], lhsT=wb[:, k, ci * 128:(ci + 1) * 128],
                            rhs=self.xbf[:, k, sl], start=(k == 0), stop=(k == KD - 1), _noinc=(k != KD - 1))
                    tf, rtf = self.tmpf.next()
                    S.I("act", "activation", [rpa], [rtf], out=tf[:], in_=pa[:], func=AF.Silu)
                    S.I("dve", "tensor_tensor", [rpb, rtf], [self.R_hT[ci][s]], out=self.hT[:, ci, sl], in0=pb[:], in1=tf[:],
                        op=ALU.mult)
            for dc in range(KD):
                for s in range(NSP):
                    sl = slice(s * SP, (s + 1) * SP)
                    py, rpy = self.psum.next()
                    for ci in range(nf):
                        S.I("pe", "matmul", [rd, self.R_hT[ci][s]], [rpy], py[:], lhsT=wd[:, ci, dc * 128:(dc + 1) * 128],
                            rhs=self.hT[:, ci, sl], start=(ci == 0), stop=(ci == nf - 1), _noinc=(ci != nf - 1))
                    S.I("dve", "scalar_tensor_tensor", [rpy, self.R_xres[dc][s]], [self.R_xres[dc][s]],
                        out=self.xres[:, dc, sl], in0=py[:], scalar=cffn, in1=self.xres[:, dc, sl],
                        op0=ALU.mult, op1=ALU.add)
        hts = [r for row in self.R_hT for r in row]
        S.inherit(self.fwork_res(exclude=hts), hts)

    def layernorm(self, l, i):
        S = self.S
        ones = self.cb("ones_ln")
        eps = self.cf("eps_ln")
        base = ((l * 3 + i) * 2) * 8
        RC = self.R_const

        def stage_a(s):
            sl = slice(s * SP, (s + 1) * SP)
            rx = [self.R_xres[c][s] for c in range(KD)]
            S.I("act", "copy", rx, [self.R_zbf], out=self.zbf, in_=self.xres[:, :, sl])
            S.I("act", "activation", rx, [self.R_zsq], out=self.zsq, in_=self.xres[:, :, sl], func=AF.Square)
            pm, rpm = self.psum.next()
            pq, rpq = self.psum.next()
            for k in range(KD):
                S.I("pe", "matmul", [self.R_zbf, RC], [rpm], pm[:], lhsT=ones, rhs=self.zbf[:, k, :],
                    start=(k == 0), stop=(k == KD - 1), _noinc=(k != KD - 1))
            for k in range(KD):
                S.I("pe", "matmul", [self.R_zsq, RC], [rpq], pq[:], lhsT=ones, rhs=self.zsq[:, k, :],
                    start=(k == 0), stop=(k == KD - 1), _noinc=(k != KD - 1))
            mean, rmean = self.tmpf.next()
            rstd, rrstd = self.tmpf.next()
            S.I("act", "copy", [rpm], [rmean], out=mean[:], in_=pm[:])
            S.I("dve", "tensor_tensor", [rmean], [rrstd], out=rstd[:], in0=mean[:], in1=mean[:], op=ALU.mult)
            S.I("dve", "tensor_tensor", [rpq, rrstd], [rrstd], out=rstd[:], in0=pq[:], in1=rstd[:], op=ALU.subtract)
            S.I("act", "activation", [rrstd, RC], [rrstd], out=rstd[:], in_=rstd[:], func=AF.Ln, bias=eps)
            S.I("act", "activation", [rrstd], [rrstd], out=rstd[:], in_=rstd[:], func=AF.Exp, scale=-0.5)
            return (s, sl, mean, rmean, rstd, rrstd)

        def stage_b(st):
            s, sl, mean, rmean, rstd, rrstd = st
            for c in range(KD):
                t2, rt2 = self.tmpb32.next()
                rxc = self.R_xres[c][s]
                S.I("dve", "tensor_tensor", [rxc, rmean], [rt2], out=t2[:], in0=self.xres[:, c, sl], in1=mean[:], op=ALU.subtract)
                S.I("dve", "tensor_tensor", [rt2, rrstd], [rt2], out=t2[:], in0=t2[:], in1=rstd[:], op=ALU.mult)
                S.I("act", "activation", [rt2, RC], [rxc], out=self.xres[:, c, sl], in_=t2[:], func=AF.Identity,
                    scale=self.lnp[:, base + c:base + c + 1], bias=self.lnp[:, base + 8 + c:base + 8 + c + 1])
                S.I("act", "copy", [rxc], [self.R_xbf[c][s]], out=self.xbf[:, c, sl], in_=self.xres[:, c, sl])
        prev = stage_a(0)
        for s in range(1, NSP):
            cur = stage_a(s)
            stage_b(prev)
            prev = cur
        stage_b(prev)
        S.inherit(self.fwork_res(exclude=[self.R_zbf, self.R_zsq]), [self.R_zbf, self.R_zsq])

    def proj_rope(self, l, col0, dst, rdst, dup=False, perm_d=1):
        S = self.S
        wt, wr = self.wsmall.next()
        w = wt[:, 0:1024].rearrange("p (k f) -> p k f", k=KD)
        if dup:
            src = self.d_w_in[l, :, col0:col0 + 64].rearrange("(k p) f -> p k f", p=128)
            S.D("pool", w[:, :, 0:64], src, writes=[wr])
            S.D("pool", w[:, :, 64:128], src, writes=[wr])
        else:
            src = self.d_w_in[l, :, col0:col0 + 128].rearrange("(k p) f -> p k f", p=128)
            S.D("pool", w, src, writes=[wr])
        rot = self.cb("rot")

        def stage1(s):
            sl = slice(s * SP, (s + 1) * SP)
            px, rpx = self.psum.next()
            for k in range(KD):
                S.I("pe", "matmul", [wr, self.R_xbf[k][s]], [rpx], px[:], lhsT=w[:, k, :], rhs=self.xbf[:, k, sl],
                    start=(k == 0), stop=(k == KD - 1), _noinc=(k != KD - 1))
            xb, rxb = self.tmpb.next()
            S.I("act", "copy", [rpx], [rxb], out=xb[:], in_=px[:])
            return (s, sl, px, rpx, xb, rxb)

        def stage2(st):
            s, sl, px, rpx, xb, rxb = st
            pr, rpr = self.psum.next()
            S.I("pe", "matmul", [rxb, self.R_const], [rpr], pr[:], lhsT=rot, rhs=xb[:], start=True, stop=True)
            t1, r1 = self.tmpf.next()
            t2, r2 = self.tmpf.next()
            S.I("dve", "tensor_tensor", [rpx, self.R_rope[s]], [r1], out=t1[:], in0=px[:], in1=self.cosT[:, sl], op=ALU.mult)
            S.I("dve", "tensor_tensor", [rpr, self.R_rope[s]], [r2], out=t2[:], in0=pr[:], in1=self.sinT[:, sl], op=ALU.mult)
            if perm_d == 1 or not DESTRIDE:
                S.I("dve", "tensor_tensor", [r1, r2], [rdst[s]], out=dst[:, sl], in0=t1[:], in1=t2[:], op=ALU.add)
            else:
                n_i = SP // perm_d
                o3 = dst.rearrange("p (r i) -> p r i", r=perm_d)[:, :, s * n_i:(s + 1) * n_i].rearrange("p r i -> p i r")
                S.I("dve", "tensor_tensor", [r1, r2], [rdst[s]], out=o3, in0=t1[:].rearrange("p (i r) -> p i r", r=perm_d),
                    in1=t2[:].rearrange("p (i r) -> p i r", r=perm_d), op=ALU.add)
        prev = stage1(0)
        for s in range(1, NSP):
            cur = stage1(s)
            stage2(prev)
            prev = cur
        stage2(prev)

    def proj_tok(self, l, col0, ncols, dram_dst, row0, rdst, dcol0, halo=None, keep=None):
        S = self.S
        mid = ncols * KD > 2048
        w, wr = self.load_w(self.d_w_in[l, :, col0:col0 + ncols], ncols, KD, mid=mid)
        for t in range(NT // 128):
            s = t // 4
            ts = slice(t * 128, (t + 1) * 128)
            pv, rpv = self.psum.next()
            for k in range(KD):
                S.I("pe", "matmul", [wr, self.R_xbf[k][s]], [rpv], pv[:, 0:ncols], lhsT=self.xbf[:, k, ts], rhs=w[:, k, :],
                    start=(k == 0), stop=(k == KD - 1), _noinc=(k != KD - 1))
            if keep is not None:
                kt, rkl = keep
                rk = rkl[t]
                S.I("act", "copy", [rpv], [rk], out=kt[:, t, :], in_=pv[:, 0:ncols])
                src = kt[:, t, :]
                rsrc = rk
            else:
                vt, rvt = self.tmpb.next()
                S.I("act", "copy", [rpv], [rvt], out=vt[:, 0:ncols], in_=pv[:, 0:ncols])
                src = vt[:, 0:ncols]
                rsrc = rvt
            S.D("sp", dram_dst[row0 + t * 128:row0 + (t + 1) * 128, dcol0:dcol0 + ncols], src, reads=[rsrc], writes=[rdst])
            if halo is not None:
                ho, g = halo
                nrows = HLB if g == "b" else HL[g]
                if t * 128 >= NT - nrows:
                    row = t * 128 - (NT - nrows)
                    if g == "b":
                        hd, hr = ho["vb"], row
                        rh = self.R_hout_l[l][6]
                    else:
                        ti, hr = vloc(g, row)
                        hd = ho["va"][ti]
                        rh = self.R_hout_l[l][3 + ti]
                    S.D("sp", hd[hr:hr + 128, dcol0:dcol0 + ncols], src, reads=[rsrc], writes=[rh])

    def m1(self, l):
        S = self.S
        ho = self.hout.get(l)
        cut = self.dbg or ""
        if cut == "m1_0":
            return
        for g in range(3):
            oq, ok, ov = OFF_A[g]
            for j in range(6):
                if cut == "m1_1" and (g, j) != (0, 0):
                    continue
                kt, rk = self.kq4.next()
                dl = DIL[g] if DESTRIDE else 1
                hseg, Lseg = HL[g] // dl, NT // dl
                self.proj_rope(l, ok + 128 * j, kt, rk, perm_d=dl)
                kt3 = kt.rearrange("p (r i) -> p r i", r=dl)
                kd3 = self.kTd_A[g][j].rearrange("p (r e) -> p r e", r=dl)
                S.D("sp", kd3[:, :, hseg:hseg + Lseg], kt3, reads=rk, writes=[self.R_kTdA[g][j]])
                if ho is not None:
                    ti, c0 = kloc(g, j)
                    S.D("sp", ho["k"][ti][:, c0:c0 + HL[g]].rearrange("p (r e) -> p r e", r=dl), kt3[:, :, Lseg - hseg:Lseg],
                        reads=rk, writes=[self.R_hout_l[l][ti]])
            hal = (ho, g) if ho is not None else None
            if cut == "m1_1":
                return
            if cut == "m1_2" and g > 0:
                return
            self.proj_tok(l, ov, 512, self.Vd_A[g], HL[g], self.R_VdA[g], 0, halo=hal)
            self.proj_tok(l, ov + 512, 256, self.Vd_A[g], HL[g], self.R_VdA[g], 512, halo=hal)
        for kv in range(2):
            kt, rk = self.kq4.next()
            self.proj_rope(l, OFF_BK + 64 * kv, kt, rk, dup=True)
            S.D("sp", self.kTd_B[kv, :, HLB:HLB + NT], kt, reads=rk, writes=[self.R_kTdB[kv]])
            if ho is not None:
                ti, c0 = kloc_b(kv)
                S.D("sp", ho["k"][ti][:, c0:c0 + HLB], kt[:, NT - HLB:NT], reads=rk, writes=[self.R_hout_l[l][ti]])
        hal = (ho, "b") if ho is not None else None
        self.proj_tok(l, OFF_BV, 128, self.Vd_B, HLB, self.R_VdB, 0, halo=hal)
        self.exchange_part(l, list(range(7)))
        if cut == "m1_3":
            return
        vC = self.vx[:, 0:4096].rearrange("p (n c) -> p n c", n=16)
        for j in range(4):
            kt, rk = self.kq4.next()
            self.proj_rope(l, OFF_CK + 128 * j, kt, rk)
            S.D("sp", self.kTd_C[j], kt, reads=rk, writes=[self.R_kTdC[j]])
            self.proj_tok(l, OFF_CV + 256 * j, 256, self.Vd_C, 0, self.R_VdC[j], 256 * j, keep=(vC, self.R_vCt))
            if ho is not None:
                self.retention_pair(l, j, kt, rk, vC, None, None, None, rv=self.R_vCt)
                S.D("sp", ho["s"][j], self.Sf[:], reads=[self.R_Sf], writes=[self.R_hout_l[l][7]])

    def exchange_part(self, l, idxs):
        S = self.S
        groups = [[0, 1], [2, 3], [4, 5], [6, 7]]
        pairs = self.cc_pairs[l]
        if getattr(self, "_hin_layer", None) != l:
            self._hin_layer = l
            self.R_hin_l = [Res() for _ in pairs]
            self.R_hin = self.R_hin_l[len(pairs) - 1]
        for i in idxs:
            src, dst = pairs[i]
            S.coll(lambda e, src=src, dst=dst: e.collective_compute(
                "AllGather", ALU.bypass, replica_groups=groups, ins=[src.opt()], outs=[dst.opt()]),
                reads=[self.R_hout_l[l][i]], writes=[self.R_hin_l[i]])

    def exchange(self, l):
        self.exchange_part(l, [7])

    def exchange_copy(self, l):
        S = self.S
        hi = self.hin[l]
        nk = len(HK_T)
        for g in range(3):
            for j in range(6):
                ti, c0 = kloc(g, j)
                dl = DIL[g] if DESTRIDE else 1
                S.D("sp", self.kTd_A[g][j].rearrange("p (r e) -> p r e", r=dl)[:, :, 0:HL[g] // dl],
                    hi["k"][ti][:, c0:c0 + HL[g]].rearrange("p (r e) -> p r e", r=dl), reads=[self.R_hin_l[ti]],
                    writes=[self.R_kTdA[g][j]])
            row = 0
            while row < HL[g]:
                ti, hr = vloc(g, row)
                n = min(HL[g] - row, HV_T[ti][0] - hr)
                S.D("sp", self.Vd_A[g][row:row + n, :], hi["va"][ti][hr:hr + n, :], reads=[self.R_hin_l[nk + ti]],
                    writes=[self.R_VdA[g]])
                row += n
        for kv in range(2):
            ti, c0 = kloc_b(kv)
            S.D("sp", self.kTd_B[kv, :, 0:HLB], hi["k"][ti][:, c0:c0 + HLB], reads=[self.R_hin_l[ti]], writes=[self.R_kTdB[kv]])
        S.D("sp", self.Vd_B[0:HLB, :], hi["vb"], reads=[self.R_hin_l[nk + len(HV_T)]], writes=[self.R_VdB])

    def unit_qk(self, job, r, blk):
        S = self.S
        d = job["d"]
        qT, rq, kx, rkx = job["qT"], job["rq"], job["kx"], job["rkx"]
        q0 = r + d * 128 * blk
        qsl = slice(q0, q0 + 127 * d + 1, d)
        spans = list(range(q0 // SP, (q0 + 127 * d) // SP + 1))
        rqs = [rq[s] for s in spans]
        mmask = job["mmask0"] if blk == 0 else job["mmask"]
        pt, rpt = self.tmpb.next()
        kx3 = kx[:, 0:job["klen"]].rearrange("p (r e) -> p r e", r=d)
        qT3 = qT.rearrange("p (r i) -> p r i", r=d)
        for hh in range(2):
            ps, rps = self.psum.next()
            hs = slice(64 * hh, 64 * hh + 64)
            for tt in range(2):
                b1 = blk + tt
                if DESTRIDE or d == 1:
                    kap = kx3[hs, r, 128 * b1:128 * b1 + 128]
                    qap = qT3[hs, r, 128 * blk:128 * blk + 128]
                else:
                    k0 = r + d * 128 * b1
                    kap = kx[hs, k0:k0 + 127 * d + 1:d]
                    qap = qT[hs, qsl]
                S.I("pe", "matmul", [rkx] + rqs, [rps], ps[:, tt * 128:tt * 128 + 128], lhsT=kap, rhs=qap, start=True, stop=True,
                    _noinc=(tt == 0))
            S.I("act", "activation", [rps], [rpt], out=pt[:, 256 * hh:256 * hh + 256], in_=ps[:, 0:256], func=AF.Exp, scale=0.125)
        S.I("dve", "tensor_tensor", [rpt, self.R_const], [rpt], out=pt[:], in0=pt[:], in1=mmask, op=ALU.mult)
        return (job, r, blk, qsl, spans, pt, rpt)

    def unit_pv(self, st):
        S = self.S
        job, r, blk, qsl, spans, pt, rpt = st
        nb1 = job["nb1"]
        vt = job["vx"].rearrange("p (n c) -> p n c", c=192)
        rvx = job["rvx"]
        p2, rp2 = self.psum.next()
        for hh in range(2):
            for tt in range(2):
                c0 = (2 * hh + tt) * 128
                ti = r * nb1 + blk + tt
                S.I("pe", "matmul", [rvx, rpt], [rp2], p2[:, 128 * hh:128 * hh + 128], lhsT=vt[:, ti, 64 * hh:64 * hh + 128],
                    rhs=pt[:, c0:c0 + 128], start=(tt == 0), stop=(tt == 1), _noinc=not (hh == 1 and tt == 1))
        d = job["d"]
        q0 = r + d * 128 * blk
        blocks = range(q0 // 128, (q0 + 127 * d) // 128 + 1)
        rhos = [(r + d * m) % 16 for m in range(max(1, 16 // d))] if d > 1 else list(range(16))
        racc = [self.R_accf[b][rho] for b in blocks for rho in rhos]
        dst = self.acc[:, :, qsl]
        src = p2[:, 0:256].rearrange("p (n t) -> p n t", n=2)
        if job["first"]:
            S.I("act", "copy", [rp2], racc, out=dst, in_=src)
        else:
            S.I("dve", "tensor_tensor", [rp2] + racc, racc, out=dst, in0=src, in1=dst, op=ALU.add)

    def job_prep(self, l, job):
        S = self.S
        qT, rq = self.kq.next()
        self.proj_rope(l, job["qcol"], qT, rq, perm_d=job["d"])
        (kx, rkx), (vx, rvx) = self.kvr.next()
        d, nb1 = job["d"], job["nb1"]
        S.D("sp", kx[:, 0:job["klen"]], job["kT_src"], reads=[job["rk_src"]], writes=[rkx])
        vt = vx.rearrange("p (n c) -> p n c", c=192)
        Vd, vcol0, shared = job["Vd"], job["vcol0"], job["shared_kv"]
        vt4 = vt[:, 0:d * nb1, :].rearrange("p (r b) c -> p b r c", r=d)
        for hh in range(2):
            vc = vcol0 if shared else vcol0 + 64 * hh
            if d > nb1:
                src4 = Vd[0:d * 128 * nb1, vc:vc + 64].rearrange("(b i r) c -> i b r c", i=128, r=d)
                for b in range(nb1):
                    S.D("sp", vt4[:, b, :, 128 * hh:128 * hh + 64], src4[:, b, :, :], reads=[job["rV"]], writes=[rvx])
            else:
                for r in range(d):
                    rows = slice(r, r + d * (128 * nb1 - 1) + 1, d)
                    src = Vd[rows, vc:vc + 64].rearrange("(b i) c -> i b c", i=128)
                    S.D("sp", vt[:, r * nb1:(r + 1) * nb1, 128 * hh:128 * hh + 64], src, reads=[job["rV"]], writes=[rvx])
        job.update(qT=qT, rq=rq, kx=kx, rkx=rkx, vx=vx, rvx=rvx)

    def attention_jobs(self, l):
        maskA, maskA0 = self.cb("maskA"), self.cb("maskA0")
        maskB, maskB0 = self.cb("maskB"), self.cb("maskB0")
        jobs = []
        for j in range(8):
            kv = j // 4
            jobs.append(dict(d=1, nblk=16, nb1=17, qcol=OFF_BQ + 128 * j, kT_src=self.kTd_B[kv], rk_src=self.R_kTdB[kv],
                             klen=HLB + NT, Vd=self.Vd_B, rV=self.R_VdB, vcol0=64 * kv, shared_kv=True, mask=maskB,
                             mask0=maskB0, mmask=self.cb("mmaskB"), mmask0=self.cb("mmaskB0"), first=True, finish=(6 + j, self.esk[:, j:j + 1])))
        for j in range(6):
            for g in range(3):
                d = DIL[g]
                jobs.append(dict(d=d, nblk=16 // d, nb1=16 // d + 1, qcol=OFF_A[g][0] + 128 * j, kT_src=self.kTd_A[g][j],
                                 rk_src=self.R_kTdA[g][j], klen=HL[g] + NT, Vd=self.Vd_A[g], rV=self.R_VdA[g], vcol0=128 * j,
                                 shared_kv=False, mask=maskA, mask0=maskA0, mmask=self.cb("mmaskA"), mmask0=self.cb("mmaskA0"),
                                 first=(g == 0),
                                 finish=((j, None) if g == 2 else None)))
        for (kb, vb) in self.kvr.items:
            vt = vb[0].rearrange("p (n c) -> p n c", c=192)
            self.S.I("pool", "memset", [], [vb[1]], vt[:, :, 64:128], 1.0)
        self.job_prep(l, jobs[0])
        DEPTH_Q = 2
        pend = []

        def pop_one():
            st, fin = pend.pop(0)
            self.unit_pv(st)
            if fin is not None:
                self.attn_finish(fin[0], extra_den=fin[1])
        for idx, job in enumerate(jobs):
            first_unit = True
            nun = job["d"] * job["nblk"]
            ui = 0
            for r in range(job["d"]):
                for blk in range(job["nblk"]):
                    cur = self.unit_qk(job, r, blk)
                    ui += 1
                    pend.append((cur, job["finish"] if ui == nun else None))
                    if len(pend) > DEPTH_Q:
                        pop_one()
                    if first_unit and all(p[0][0] is job for p in pend):
                        first_unit = False
                        if idx + 1 < len(jobs):
                            self.job_prep(l, jobs[idx + 1])
            assert not first_unit
        while pend:
            pop_one()

    def attn_finish(self, ychunk, extra_den=None):
        S = self.S
        RC = self.R_const
        shdn, shup = self.cf("shdn"), self.cf("shup")
        yt, ry = self.kq2.next()
        for s in range(NSP):
            sl = slice(s * SP, (s + 1) * SP)
            pd, rpd = self.psum.next()
            S.I("pe", "matmul", self.R_acc[s] + [RC], [rpd], pd[:], lhsT=shdn, rhs=self.acc[:, 0, sl], start=True, stop=False)
            S.I("pe", "matmul", self.R_acc[s] + [RC], [rpd], pd[:], lhsT=shup, rhs=self.acc[:, 1, sl], start=False, stop=True)
            tf, rtf = self.tmpf.next()
            if extra_den is not None:
                S.I("act", "activation", [rpd, self.R_esk], [rtf], out=tf[:], in_=pd[:], func=AF.Ln, bias=extra_den)
            else:
                S.I("act", "activation", [rpd], [rtf], out=tf[:], in_=pd[:], func=AF.Ln)
            S.I("act", "activation", [rtf], [rtf], out=tf[:], in_=tf[:], func=AF.Exp, scale=-1.0)
            for hh in range(2):
                hs = slice(64 * hh, 64 * hh + 64)
                S.I("dve", "tensor_tensor", self.R_acc[s] + [rtf], [ry[s]], out=yt[hs, sl], in0=self.acc[hs, hh, sl], in1=tf[hs, :],
                    op=ALU.mult)
        S.D("sp", self.yT_d[ychunk], yt, reads=ry, writes=[self.R_yTd[ychunk]])

    def load_kv(self, kT_src, rk_src, klen, Vd, rV, d, vcol0, vcols, nb1):
        S = self.S
        S.D("sp", self.kx[:, 0:klen], kT_src, reads=[rk_src], writes=[self.R_kx])
        vt = self.vx.rearrange("p (n c) -> p n c", c=128)
        for r in range(d):
            src = Vd[r:r + d * (128 * nb1 - 1) + 1:d, vcol0:vcol0 + vcols].rearrange("(b i) c -> i b c", i=128)
            S.D("sp", vt[:, r * nb1:(r + 1) * nb1, 0:vcols], src, reads=[rV], writes=[self.R_vx])

    def m2a(self, l):
        S = self.S
        S.I("act", "activation", [self.R_const], [self.R_esk], out=self.esk[:], in_=self.sk[:, l * 8:(l + 1) * 8], func=AF.Exp)
        vC = self.vx[:, 0:4096].rearrange("p (n c) -> p n c", n=16)
        for j in range(4):
            qT, rq = self.kq.next()
            self.proj_rope(l, OFF_CQ + 128 * j, qT, rq)
            S.D("sp", self.kx[:, 0:NT], self.kTd_C[j], reads=[self.R_kTdC[j]], writes=[self.R_kx])
            src = self.Vd_C[:, 256 * j:256 * j + 256].rearrange("(n i) c -> i n c", i=128)
            S.D("sp", vC, src, reads=[self.R_VdC[j]], writes=[self.R_vx])
            self.retention_pair(l, j, self.kx[:, 0:NT], [self.R_kx] * NSP, vC, qT, rq, self.hin[l]["s"][j])
            self.retention_finish(l, j)
        self.exchange_copy(l)
        self.attention_jobs(l)

    def retention_pair(self, l, j, kT, rk, vC, qT, rq, s0_dram, rv=None):
        S = self.S
        if rv is None:
            rv = [self.R_vx] * 16
        RC = self.R_const
        intra = self.cf("intra")[:, 256 * j:256 * j + 256]
        qdec = self.cf("qdec")[:, 128 * j:128 * j + 128]
        kdec = self.cf("kdec")[:, 128 * j:128 * j + 128]
        cdec = self.cf("cdec")[:, j:j + 1]
        ident = self.cb("ident")
        outputs = qT is not None
        if s0_dram is not None:
            S.D("sp", self.Sf[:], s0_dram, reads=[self.R_hin], writes=[self.R_Sf])
            S.I("dve", "tensor_scalar", [self.R_Sf, RC], [self.R_Sf], out=self.Sf[:], in0=self.Sf[:], scalar1=self.cf("hv"),
                scalar2=None, op0=ALU.mult)
        else:
            S.I("dve", "memset", [], [self.R_Sf], self.Sf[:], 0.0)
        S.I("act", "copy", [self.R_Sf], [self.R_Sb], out=self.Sb[:], in_=self.Sf[:])
        if outputs:
            qd, rqd = self.kq2.next()
            for s in range(NSP):
                sl = slice(s * SP, (s + 1) * SP)
                S.I("dve", "tensor_tensor", [rq[s], RC], [rqd[s]], out=qd[:, sl].rearrange("p (n t) -> p n t", t=128),
                    in0=qT[:, sl].rearrange("p (n t) -> p n t", t=128),
                    in1=qdec.unsqueeze(1).broadcast_to([128, 4, 128]), op=ALU.mult)
        def stage1(n):
            s = n // 4
            cs = slice(n * 128, (n + 1) * 128)
            pT, rpT = self.psum.next()
            pTb = pT[:].bitcast(BF16)
            S.I("pe", "transpose", [rk[s], RC], [rpT], pTb[:, 0:128], kT[:, cs], ident)
            kd, rkd = self.tmpb.next()
            S.I("dve", "tensor_tensor", [rpT, RC], [rkd], out=kd[:, 0:128], in0=pTb[:, 0:128], in1=kdec, op=ALU.mult)
            aT = raT = None
            if outputs:
                aT, raT = self.tmpb.next()
                for hh in range(2):
                    hs = slice(64 * hh, 64 * hh + 64)
                    pa, rpa = self.psum.next()
                    S.I("pe", "matmul", [rk[s], rq[s]], [rpa], pa[:, 0:128], lhsT=kT[hs, cs], rhs=qT[hs, cs], start=True, stop=True)
                    S.I("dve", "tensor_tensor", [rpa, RC], [raT], out=aT[:, 128 * hh:128 * hh + 128], in0=pa[:, 0:128],
                        in1=intra[:, 128 * hh:128 * hh + 128], op=ALU.mult)
            return (n, s, cs, kd, rkd, aT, raT)

        def stage2(st):
            n, s, cs, kd, rkd, aT, raT = st
            if outputs:
                for hh in range(2):
                    hs = slice(64 * hh, 64 * hh + 64)
                    po, rpo = self.psum.next()
                    S.I("pe", "matmul", [rv[n], raT], [rpo], po[:, 0:128],
                        lhsT=vC[:, n, 128 * hh:128 * hh + 128], rhs=aT[:, 128 * hh:128 * hh + 128], start=True, stop=False)
                    S.I("pe", "matmul", [self.R_Sb, rqd[s]], [rpo], po[:, 0:128],
                        lhsT=self.Sb[hs, :], rhs=qd[hs, cs], start=False, stop=True)
                    S.I("act", "copy", [rpo], self.R_acc[s], out=self.acc[:, hh, cs], in_=po[:, 0:128])
            pu, rpu = self.psum.next()
            for hh in range(2):
                hs = slice(64 * hh, 64 * hh + 64)
                S.I("pe", "matmul", [rkd, rv[n]], [rpu], pu[hs, 0:128], lhsT=kd[:, hs], rhs=vC[:, n, 128 * hh:128 * hh + 128],
                    start=True, stop=True)
            S.I("dve", "scalar_tensor_tensor", [self.R_Sf, rpu, RC], [self.R_Sf], out=self.Sf[:], in0=self.Sf[:], scalar=cdec,
                in1=pu[:, 0:128], op0=ALU.mult, op1=ALU.add)
            if outputs:
                S.I("act", "copy", [self.R_Sf], [self.R_Sb], out=self.Sb[:], in_=self.Sf[:])
        NCH = NT // 128
        cur = stage1(0)
        for n in range(NCH):
            nxt = stage1(n + 1) if n + 1 < NCH else None
            stage2(cur)
            cur = nxt

    def retention_finish(self, l, j):
        S = self.S
        RC = self.R_const
        ones = self.cb("ones_gn")
        eps = self.cf("eps_gn")

        def stage_a(hh, s, wg, rwg):
            sl = slice(s * SP, (s + 1) * SP)
            r_ap = self.acc[:, hh, sl]
            rb_, rrb = self.tmpb.next()
            rs_, rrs = self.tmpb.next()
            S.I("act", "copy", self.R_acc[s], [rrb], out=rb_[:], in_=r_ap)
            S.I("act", "activation", self.R_acc[s], [rrs], out=rs_[:], in_=r_ap, func=AF.Square)
            pm, rpm = self.psum.next()
            pq, rpq = self.psum.next()
            S.I("pe", "matmul", [rrb, RC], [rpm], pm[:], lhsT=ones, rhs=rb_[:], start=True, stop=True)
            S.I("pe", "matmul", [rrs, RC], [rpq], pq[:], lhsT=ones, rhs=rs_[:], start=True, stop=True)
            pg, rpg = self.psum.next()
            for k in range(KD):
                S.I("pe", "matmul", [rwg, self.R_xbf[k][s]], [rpg], pg[:], lhsT=wg[:, k, :], rhs=self.xbf[:, k, sl],
                    start=(k == 0), stop=(k == KD - 1), _noinc=(k != KD - 1))
            m_, rm = self.tmpf.next()
            v_, rv = self.tmpf.next()
            S.I("act", "copy", [rpm], [rm], out=m_[:], in_=pm[:])
            S.I("dve", "tensor_tensor", [rm], [rv], out=v_[:], in0=m_[:], in1=m_[:], op=ALU.mult)
            S.I("dve", "tensor_tensor", [rpq, rv], [rv], out=v_[:], in0=pq[:], in1=v_[:], op=ALU.subtract)
            return (hh, s, sl, r_ap, m_, rm, v_, rv, pg, rpg)

        def stage_b(st, yt, ry):
            hh, s, sl, r_ap, m_, rm, v_, rv, pg, rpg = st
            t_, rt = self.tmpf.next()
            g_, rg = self.tmpf.next()
            S.I("act", "activation", [rv, RC], [rv], out=v_[:], in_=v_[:], func=AF.Ln, bias=eps)
            S.I("act", "activation", [rv], [rv], out=v_[:], in_=v_[:], func=AF.Exp, scale=-0.5)
            S.I("act", "activation", [rpg], [rg], out=g_[:], in_=pg[:], func=AF.Silu)
            S.I("dve", "tensor_tensor", self.R_acc[s] + [rm], [rt], out=t_[:], in0=r_ap, in1=m_[:], op=ALU.subtract)
            S.I("dve", "tensor_tensor", [rt, rv], [rt], out=t_[:], in0=t_[:], in1=v_[:], op=ALU.mult)
            S.I("dve", "tensor_tensor", [rt, rg], [ry[s]], out=yt[:, sl], in0=t_[:], in1=g_[:], op=ALU.mult)
        for hh in range(2):
            hd = 2 * j + hh
            wg, rwg = self.load_w(self.d_w_in[l, :, OFF_CG + 128 * hd:OFF_CG + 128 * hd + 128], 128, KD)
            yt, ry = self.kq2.next()
            prev = stage_a(hh, 0, wg, rwg)
            for s in range(1, NSP):
                cur = stage_a(hh, s, wg, rwg)
                stage_b(prev, yt, ry)
                prev = cur
            stage_b(prev, yt, ry)
            S.D("sp", self.yT_d[14 + hd], yt, reads=ry, writes=[self.R_yTd[14 + hd]])

    def m2b(self, l):
        S = self.S
        RC = self.R_const
        branches = ((0, 6, 0, self.d_wpa), (1, 8, 6, self.d_wpb), (2, 8, 14, self.d_wpc))
        for (X, KX, yb, wp_d) in branches:
          for h in range(2):
            c0 = OFF_G + 1024 * X + 512 * h
            wg, rwg = self.load_w(self.d_w_in[l, :, c0:c0 + 512], 512, KD, mid=True)
            wp, rwp = self.load_w(wp_d[l, :, 512 * h:512 * h + 512], 512, KX, mid=True)
            for s in range(NSP):
                sl = slice(s * SP, (s + 1) * SP)
                yt, ry = self.ysp.next()
                ysp = yt[:, 0:KX * SP].rearrange("p (k t) -> p k t", k=KX)
                S.D("sp", ysp, self.yT_d[yb:yb + KX, :, sl].rearrange("k p t -> p k t"),
                    reads=[self.R_yTd[yb + k] for k in range(KX)], writes=[ry])
                for dcl in range(4):
                    dc = 4 * h + dcl
                    dsl = slice(dcl * 128, (dcl + 1) * 128)
                    pg, rpg = self.psum.next()
                    for k in range(KD):
                        S.I("pe", "matmul", [rwg, self.R_xbf[k][s]], [rpg], pg[:], lhsT=wg[:, k, dsl], rhs=self.xbf[:, k, sl],
                            start=(k == 0), stop=(k == KD - 1), _noinc=(k != KD - 1))
                    g_, rg = self.tmpf.next()
                    bi = l * 24 + 8 * X + dc
                    S.I("act", "activation", [rpg, RC], [rg], out=g_[:], in_=pg[:], func=AF.Sigmoid, bias=self.gb[:, bi:bi + 1])
                    pp, rpp = self.psum.next()
                    for k in range(KX):
                        S.I("pe", "matmul", [rwp, ry], [rpp], pp[:], lhsT=wp[:, k, dsl], rhs=ysp[:, k, :],
                            start=(k == 0), stop=(k == KX - 1), _noinc=(k != KX - 1))
                    rm = self.R_merged[dc][s]
                    if X == 0:
                        S.I("dve", "tensor_tensor", [rpp, rg], [rm], out=self.merged[:, dc, sl], in0=pp[:], in1=g_[:], op=ALU.mult)
                    else:
                        S.I("dve", "tensor_tensor", [rpp, rg], [rg], out=g_[:], in0=pp[:], in1=g_[:], op=ALU.mult)
                        S.I("dve", "tensor_tensor", [rg, rm], [rm], out=self.merged[:, dc, sl], in0=g_[:],
                            in1=self.merged[:, dc, sl], op=ALU.add)
        S.barrier()
        wo, rwo = self.load_w(self.d_wout[l], 1024, KD, big=True)
        for s in range(NSP):
            sl = slice(s * SP, (s + 1) * SP)
            rms = [self.R_merged[c][s] for c in range(KD)]
            S.I("act", "copy", rms, [self.R_zbf], out=self.zbf, in_=self.merged[:, :, sl])
            pos = []
            for dc in range(KD):
                po, rpo = self.psum.next()
                for k in range(KD):
                    S.I("pe", "matmul", [rwo, self.R_zbf], [rpo], po[:], lhsT=wo[:, k, dc * 128:(dc + 1) * 128],
                        rhs=self.zbf[:, k, :], start=(k == 0), stop=(k == KD - 1), _noinc=(k != KD - 1))
                t_, rt = self.tmpf.next()
                S.I("act", "copy", [rpo], [rt], out=t_[:], in_=po[:])
                S.D("sp", self.mixd[dc, :, sl], t_[:], reads=[rt], writes=[self.R_mixd[s]])
        S.barrier()
        for s in range(NSP):
            sl = slice(s * SP, (s + 1) * SP)
            S.D("sp", self.xres[:, :, sl], self.xsp[:, :, sl], reads=[self.R_xsp[s]], writes=[self.R_xres[c][s] for c in range(KD)])
            for c in range(KD):
                t_, rt = self.tmpf.next()
                S.D("sp", t_[:], self.mixd[c, :, sl], reads=[self.R_mixd[s]], writes=[rt])
                S.I("dve", "scalar_tensor_tensor", [rt, self.R_xres[c][s]], [self.R_xres[c][s]], out=self.xres[:, c, sl],
                    in0=t_[:], scalar=1.0 / ALPHA, in1=self.xres[:, c, sl], op0=ALU.mult, op1=ALU.add)
        S.barrier()


def _small_params(ln1_g, ln1_b, ln2_g, ln2_b, ln3_g, ln3_b, gate_bias, attn_sinks):
    L = DEPTH
    lnp = np.zeros((128, L, 3, 2, 8), np.float32)
    for l in range(L):
        for i, (g, b) in enumerate(((ln1_g, ln1_b), (ln2_g, ln2_b), (ln3_g, ln3_b))):
            lnp[:, l, i, 0, :] = np.asarray(g[l], np.float32).reshape(8, 128).T
            lnp[:, l, i, 1, :] = np.asarray(b[l], np.float32).reshape(8, 128).T
    gb = np.zeros((128, L, 24), np.float32)
    sk = np.zeros((128, L, 8), np.float32)
    for l in range(L):
        gb[:, l, :] = np.asarray(gate_bias[l], np.float32).reshape(24, 128).T
        s = np.asarray(attn_sinks[l], np.float32).reshape(8, 2)
        sk[:64, l, :] = s[:, 0][None, :]
        sk[64:, l, :] = s[:, 1][None, :]
    return lnp.reshape(128, -1), gb.reshape(128, -1), sk.reshape(128, -1)


def make_in_maps(inputs, cores):
    x = np.asarray(inputs["x"], np.float32)
    pos = np.asarray(inputs["positions"], np.int32)
    lnp, gb, sk = _small_params(inputs["ln1_g"], inputs["ln1_b"], inputs["ln2_g"], inputs["ln2_b"],
                                inputs["ln3_g"], inputs["ln3_b"], inputs["gate_bias"], inputs["attn_sinks"])
    shared = {k: np.ascontiguousarray(np.asarray(inputs[k], np.float32)) for k in
              ("w_in", "w_proj_a", "w_proj_b", "w_proj_c", "w_out", "ffn1_up", "ffn1_down", "ffn2_up", "ffn2_down")}
    maps = []
    for c in cores:
        b, h = c // 2, c % 2
        m = dict(shared)
        m["xT"] = np.ascontiguousarray(x[b, h * NT:(h + 1) * NT, :].T)
        m["pos"] = np.ascontiguousarray(pos[b, h * NT:(h + 1) * NT].reshape(1, NT))
        m["lnp"] = lnp
        m["gbias"] = gb
        m["sinks"] = sk
        m["cbf"] = _bf_pack(h == 1)
        m["cf32"] = _f32_consts(h == 1)[0]
        maps.append(m)
    return maps


_PROGS = {}


def _get_prog(stage):
    if stage not in _PROGS:
        _PROGS[stage] = Prog(stage).build()
    return _PROGS[stage]


def kernel(**inputs):
    cores = list(range(8))
    maps = make_in_maps(inputs, cores)
    nc = _get_prog("F")
    res = run_bass_kernel_spmd(nc, maps, core_ids=cores).results
    out = np.zeros((4, 2 * NT, D), np.float32)
    for c in cores:
        b, h = c // 2, c % 2
        out[b, h * NT:(h + 1) * NT, :] = np.asarray(res[c]["outT"], np.float32).T
    return out
```

```python
import numpy as np
import ml_dtypes
from contextlib import ExitStack
import concourse.bass as bass
import concourse.mybir as mybir
from concourse.bass_utils import run_bass_kernel_spmd

F32 = mybir.dt.float32
BF16 = mybir.dt.bfloat16
I32 = mybir.dt.int32
ALU = mybir.AluOpType
AF = mybir.ActivationFunctionType

D = 1024
KD = 8
NT = 2048
SP = 512
NSP = NT // SP
DFF = 2816
NFC = DFF // 128
DEPTH = 2
ALPHA = (2.0 * DEPTH) ** 0.25
LN_EPS = 1e-5
INCOLS = 14336
A_W = 768
OFF_A = [(3 * g * A_W, (3 * g + 1) * A_W, (3 * g + 2) * A_W) for g in range(3)]
OFF_BQ = 9 * A_W
OFF_BK = OFF_BQ + 1024
OFF_BV = OFF_BK + 128
OFF_CQ = OFF_BV + 128
OFF_CK = OFF_CQ + 512
OFF_CV = OFF_CK + 512
OFF_CG = OFF_CV + 1024
OFF_G = OFF_CG + 1024
DIL = (1, 4, 16)
HL = (128, 512, 2048)
HLB = 128


class Res:
    __slots__ = ("name", "last_w", "readers", "excl")

    def __init__(self, name="", excl=False):
        self.name = name
        self.last_w = None
        self.readers = []
        self.excl = excl


class Sched:
    ENGS = ("pe", "act", "dve", "pool", "sp")

    def __init__(self, nc, stack, n_dma_sems=40):
        self.nc = nc
        self.ops = {e: [] for e in self.ENGS}
        self.count = {}
        self.sem = {}
        self.seen = {e: {} for e in self.ENGS}
        for e in self.ENGS:
            self.sem[e] = stack.enter_context(nc.semaphore("s_" + e))
            self.count[e] = 0
        self.dma_pool = []
        self.qpool = {"sp": [], "pool": [], "act": []}
        for i in range(n_dma_sems):
            k = "dma%d" % i
            self.sem[k] = stack.enter_context(nc.semaphore("s_" + k))
            self.count[k] = 0
            self.dma_pool.append(k)
            self.qpool["sp" if i < 24 else ("pool" if i < 36 else "act")].append(k)
        self.dma_rr = {"sp": 0, "pool": 0, "act": 0}
        self.cc_keys = []
        self.cc_i = 0
        for i in range(20):
            k = "cc%d" % i
            self.sem[k] = stack.enter_context(nc.semaphore("s_" + k))
            self.count[k] = 0
            self.cc_keys.append(k)
        self.out_toks = []

    def _deps(self, eng, reads, writes):
        need = {}

        def add(tok):
            if tok is None:
                return
            k, v = tok
            if need.get(k, 0) < v:
                need[k] = v
        for r in reads:
            add(r.last_w)
            if r.excl:
                for rd in r.readers:
                    add(rd)
        for w in writes:
            add(w.last_w)
            for rd in w.readers:
                add(rd)
        waits = []
        for k, v in need.items():
            if k == eng and eng == "pe":
                continue
            if self.seen[eng].get(k, 0) >= v:
                continue
            self.seen[eng][k] = v
            waits.append((k, v))
        return waits

    def _commit(self, tok, reads, writes):
        for r in reads:
            r.readers.append(tok)
            if len(r.readers) > 64:
                best = {}
                for k, v in r.readers:
                    if best.get(k, 0) < v:
                        best[k] = v
                r.readers = list(best.items())
        for w in writes:
            w.last_w = tok
            w.readers = []

    def op(self, eng, fn, reads=(), writes=(), noinc=False):
        waits = self._deps(eng, reads, writes)
        if noinc:
            assert eng == "pe"
            tok = (eng, self.count[eng] + 1)
            self.ops[eng].append((waits, fn, eng, 0))
        else:
            self.count[eng] += 1
            tok = (eng, self.count[eng])
            self.ops[eng].append((waits, fn, eng, 1))
        self._commit(tok, reads, writes)
        return tok

    def I(self, eng, name, reads, writes, *args, **kw):
        noinc = kw.pop("_noinc", False)
        return self.op(eng, lambda e: getattr(e, name)(*args, **kw), reads, writes, noinc=noinc)

    def D(self, eng, out, in_, reads=(), writes=(), is_output=False):
        return self.dma(eng, lambda e: e.dma_start(out=out, in_=in_), reads, writes, is_output)

    def dma(self, eng, fn, reads=(), writes=(), is_output=False):
        waits = self._deps(eng, reads, writes)
        pool = self.qpool[eng]
        key = pool[self.dma_rr[eng] % len(pool)]
        self.dma_rr[eng] += 1
        if self.count[key] > 0 and self.seen[eng].get(key, 0) < self.count[key]:
            self.seen[eng][key] = self.count[key]
            waits.append((key, self.count[key]))
        self.count[key] += 16
        tok = (key, self.count[key])
        self.ops[eng].append((waits, fn, key, 16))
        self._commit(tok, reads, writes)
        if is_output:
            self.out_toks.append(tok)
        return tok

    def coll(self, fn, reads=(), writes=()):
        waits = self._deps("pool", reads, writes)
        key = self.cc_keys[self.cc_i]
        self.cc_i += 1
        self.count[key] += 1
        tok = (key, self.count[key])
        self.ops["pool"].append((waits, fn, key, 1))
        self._commit(tok, reads, writes)
        return tok

    def inherit(self, new_list, old_list):
        toks = []
        for o in old_list:
            if o.last_w is not None:
                toks.append(o.last_w)
            toks.extend(o.readers)
        best = {}
        for k, v in toks:
            if best.get(k, 0) < v:
                best[k] = v
        toks = list(best.items())
        for n in new_list:
            n.readers = list(n.readers) + toks

    def barrier(self):
        snap = [(k, v) for k, v in self.count.items() if v > 0]
        for e in self.ENGS:
            waits = []
            for k, v in snap:
                if k == e:
                    continue
                if self.seen[e].get(k, 0) >= v:
                    continue
                self.seen[e][k] = v
                waits.append((k, v))
            if waits:
                self.ops[e].append((waits, None, None, 0))

    def barrier_engines(self):
        snap = [(k, v) for k, v in self.count.items() if v > 0 and not k.startswith("cc")]
        for e in self.ENGS:
            waits = []
            for k, v in snap:
                if k == e or self.seen[e].get(k, 0) >= v:
                    continue
                self.seen[e][k] = v
                waits.append((k, v))
            if waits:
                self.ops[e].append((waits, None, None, 0))

    def finish(self, eng="sp"):
        waits = [(k, self.count[k]) for k in self.dma_pool if self.count[k] > 0]
        self.ops[eng].append((waits, None, None, 0))

    def run(self, block):
        engmap = {"pe": "tensor", "act": "scalar", "dve": "vector", "pool": "gpsimd", "sp": "sync"}
        for e in self.ENGS:
            ops = self.ops[e]
            if not ops:
                continue

            def body(engine, ops=ops):
                for waits, fn, key, amt in ops:
                    for k, v in waits:
                        engine.wait_ge(self.sem[k], v)
                    if fn is not None:
                        ins = fn(engine)
                        if amt:
                            ins.then_inc(self.sem[key], amt)
            getattr(block, engmap[e])(body)


class Ring:
    def __init__(self, items):
        self.items = items
        self.i = 0

    def next(self):
        it = self.items[self.i % len(self.items)]
        self.i += 1
        return it


def _consts():
    c = {}
    c["ident"] = np.eye(128, dtype=np.float32)
    rot = np.zeros((128, 128), np.float32)
    for m in range(128):
        if m % 64 < 32:
            rot[m + 32, m] = -1.0
        else:
            rot[m - 32, m] = 1.0
    c["rot"] = rot
    c["ones_ln"] = np.full((128, 128), 1.0 / 1024.0, np.float32)
    c["ones_gn"] = np.full((128, 128), 1.0 / 128.0, np.float32)
    c["ones"] = np.ones((128, 128), np.float32)
    k = np.arange(128)[:, None]
    q = np.arange(128)[None, :]
    NEG = -30000.0
    cur = np.where(k <= q, 0.0, NEG).astype(np.float32)
    prevA = np.where(k >= q, 0.0, NEG).astype(np.float32)
    prevB = np.where(k > q, 0.0, NEG).astype(np.float32)
    c["maskA"] = np.concatenate([prevA, cur], 1)
    c["maskB"] = np.concatenate([prevB, cur], 1)
    z = np.full_like(cur, NEG)
    c["maskZ"] = np.concatenate([z, cur], 1)
    for n in ("maskA", "maskB", "maskZ"):
        m = (c[n] == 0.0).astype(np.float32)
        c["m" + n] = np.concatenate([m, m], 1)
    return c


BF_NAMES = ["ident", "rot", "ones_ln", "ones_gn", "ones", "maskA", "maskB", "maskA0", "maskB0",
            "mmaskA", "mmaskB", "mmaskA0", "mmaskB0"]
BF_W = {"ident": 128, "rot": 128, "ones_ln": 128, "ones_gn": 128, "ones": 128,
        "maskA": 256, "maskB": 256, "maskA0": 256, "maskB0": 256,
        "mmaskA": 512, "mmaskB": 512, "mmaskA0": 512, "mmaskB0": 512}


def _bf_pack(is_odd):
    c = _consts()
    c["maskA0"] = c["maskA"] if is_odd else c["maskZ"]
    c["maskB0"] = c["maskB"] if is_odd else c["maskZ"]
    c["mmaskA0"] = c["mmaskA"] if is_odd else c["mmaskZ"]
    c["mmaskB0"] = c["mmaskB"] if is_odd else c["mmaskZ"]
    return np.concatenate([c[n] for n in BF_NAMES], 1).astype(ml_dtypes.bfloat16)


def _f32_consts(is_odd):
    parts = {}
    half = 32
    inv = (10000.0 ** (-np.arange(half, dtype=np.float32) / half)).astype(np.float32)
    p = np.arange(128)
    parts["invf"] = (inv[p % 32].astype(np.float64) / (2 * np.pi)).astype(np.float32)[:, None]
    parts["hv"] = np.full((128, 1), 1.0 if is_odd else 0.0, np.float32)
    parts["eps_ln"] = np.full((128, 1), LN_EPS / ALPHA ** 2, np.float32)
    parts["eps_gn"] = np.full((128, 1), LN_EPS, np.float32)
    gam = 1.0 - 2.0 ** (-5.0 - np.arange(8, dtype=np.float64))
    lg = np.log(gam)
    intra = np.zeros((128, 4, 256), np.float64)
    qdec = np.zeros((128, 4, 128), np.float64)
    kdec = np.zeros((128, 4, 128), np.float64)
    cdec = np.zeros((128, 4), np.float64)
    kk = np.arange(128)[:, None]
    qq = np.arange(128)[None, :]
    for j in range(4):
        for hh in range(2):
            h = 2 * j + hh
            rel = qq - kk
            intra[:, j, hh * 128:(hh + 1) * 128] = np.where(rel >= 0, np.exp(lg[h] * np.maximum(rel, 0)), 0.0) * 0.125
            qdec[hh * 64:(hh + 1) * 64, j, :] = np.exp(lg[h] * (np.arange(128) + 1.0))[None, :]
            kdec[:, j, hh * 64:(hh + 1) * 64] = (np.exp(lg[h] * (127.0 - np.arange(128))) * 0.125)[:, None]
            cdec[hh * 64:(hh + 1) * 64, j] = np.exp(lg[h] * 128.0)
    shdn = np.zeros((128, 128), np.float32)
    shup = np.zeros((128, 128), np.float32)
    for m in range(64):
        shdn[m + 64, m] = 1.0
        shup[m, m + 64] = 1.0
    parts["shdn"] = shdn
    parts["shup"] = shup
    parts["intra"] = intra.reshape(128, -1)
    parts["qdec"] = qdec.reshape(128, -1)
    parts["kdec"] = kdec.reshape(128, -1)
    parts["cdec"] = cdec
    offs = {}
    o = 0
    arrs = []
    for n, a in parts.items():
        a = np.asarray(a, np.float32)
        offs[n] = (o, a.shape[1])
        o += a.shape[1]
        arrs.append(a)
    return np.concatenate(arrs, 1), offs


F32C, F32_OFFS = _f32_consts(True)
NF32C = F32C.shape[1]


HK_T = [(128, 7168), (128, 7680), (128, 1536)]
HV_T = [(1024, A_W), (1024, A_W), (640, A_W)]


def kloc(g, j):
    if g == 2:
        return (0, j * 2048) if j < 3 else (1, (j - 3) * 2048)
    if g == 0:
        return (0, 6144 + j * 128)
    return (1, 6144 + j * 512) if j < 3 else (2, (j - 3) * 512)


def kloc_b(kv):
    return (0, 6144 + 768 + kv * 128)


def vloc(g, row):
    if g == 2:
        return (0, row) if row < 1024 else (1, row - 1024)
    if g == 0:
        return (2, row)
    return (2, 128 + row)


MASK_ENG = "pool"
DESTRIDE = False


class Prog:
    def __init__(self, stage, dbg=None):
        self.stage = stage
        self.dbg = dbg
        self.nc = bass.Bass("TRN2", target_bir_lowering=False)
        self.st = ExitStack()

    def sb(self, name, shape, dt):
        return self.st.enter_context(self.nc.sbuf_tensor(name, shape, dt))

    def din(self, name, shape, dt):
        return self.nc.dram_tensor(name, shape, dt, kind="ExternalInput").ap()

    def dout(self, name, shape, dt):
        return self.nc.dram_tensor(name, shape, dt, kind="ExternalOutput").ap()

    def dint(self, name, shape, dt):
        return self.nc.dram_tensor(name, shape, dt, kind="Internal").ap()

    def build(self):
        nc = self.nc
        st = self.st
        with st:
            self._declare()
            self.S = Sched(nc, st)
            block = st.enter_context(nc.Block())
            self._program()
            self.S.finish("sp")
            self.S.run(block)
        return nc

    def _declare(self):
        nc = self.nc
        L = DEPTH
        self.d_xT = self.din("xT", [D, NT], F32)
        self.d_pos = self.din("pos", [1, NT], I32)
        self.d_w_in = self.din("w_in", [L, D, INCOLS], F32)
        self.d_wpa = self.din("w_proj_a", [L, A_W, D], F32)
        self.d_wpb = self.din("w_proj_b", [L, D, D], F32)
        self.d_wpc = self.din("w_proj_c", [L, D, D], F32)
        self.d_wout = self.din("w_out", [L, D, D], F32)
        self.d_f1u = self.din("ffn1_up", [L, D, 2 * DFF], F32)
        self.d_f1d = self.din("ffn1_down", [L, DFF, D], F32)
        self.d_f2u = self.din("ffn2_up", [L, D, 2 * DFF], F32)
        self.d_f2d = self.din("ffn2_down", [L, DFF, D], F32)
        self.d_lnp = self.din("lnp", [128, L * 3 * 2 * 8], F32)
        self.d_gb = self.din("gbias", [128, L * 24], F32)
        self.d_sk = self.din("sinks", [128, L * 8], F32)
        nbf = sum(BF_W[n] for n in BF_NAMES)
        self.d_cbf = self.din("cbf", [128, nbf], BF16)
        self.d_cf32 = self.din("cf32", [128, NF32C], F32)
        self.hin = {}
        self.hout = {}
        self.cc_pairs = {}
        for l in range(DEPTH):
            pairs = []

            def mk(name, rows, cols, dt):
                o = self.dint("%s_o%d" % (name, l), [rows, cols], dt)
                g = self.dint("%s_g%d" % (name, l), [2 * rows, cols], dt)
                pairs.append((o, g))
                return o, g[0:rows, :]
            ks = [mk("hk%d" % i, r, c, BF16) for i, (r, c) in enumerate(HK_T)]
            vs = [mk("hv%d" % i, r, c, BF16) for i, (r, c) in enumerate(HV_T)]
            vb = mk("hvb", HLB, 128, BF16)
            ss = mk("hs", 512, 128, F32)
            self.hout[l] = dict(k=[k[0] for k in ks], va=[v[0] for v in vs], vb=vb[0],
                                s=ss[0].rearrange("(j p) c -> j p c", p=128))
            self.hin[l] = dict(k=[k[1] for k in ks], va=[v[1] for v in vs], vb=vb[1],
                               s=ss[1].rearrange("(j p) c -> j p c", p=128))
            self.cc_pairs[l] = pairs
        self.R_hin = Res()
        self.d_out = self.dout("outT", [D, NT], F32)
        self.kTd_A = [self.dint("kTdA%d" % g, [6, 128, HL[g] + NT], BF16) for g in range(3)]
        self.kTd_B = self.dint("kTdB", [2, 128, HLB + NT], BF16)
        self.kTd_C = self.dint("kTdC", [4, 128, NT], BF16)
        self.Vd_A = [self.dint("VdA%d" % g, [HL[g] + NT, A_W], BF16) for g in range(3)]
        self.Vd_B = self.dint("VdB", [HLB + NT, 128], BF16)
        self.Vd_C = self.dint("VdC", [NT, 1024], BF16)
        self.yT_d = self.dint("yTd", [22, 128, NT], BF16)
        self.xsp = self.dint("xsp", [128, KD, NT], F32)
        self.mixd = self.dint("mixd", [KD, 128, NT], F32)
        self.R_mixd = [Res() for s in range(NSP)]
        self.R_kTdA = [[Res() for j in range(6)] for g in range(3)]
        self.R_kTdB = [Res(), Res()]
        self.R_kTdC = [Res() for j in range(4)]
        self.R_VdA = [Res() for g in range(3)]
        self.R_VdB = Res()
        self.R_VdC = [Res() for j in range(4)]
        self.R_yTd = [Res() for c in range(22)]
        self.R_xsp = [Res() for s in range(NSP)]
        self.R_hout = Res()
        self.XR = self.sb("XR", [128, KD * NT], F32)
        self.xres = self.XR[:].rearrange("p (c t) -> p c t", c=KD)
        self.XRb = self.XR[:].bitcast(BF16)
        self.xbf = self.sb("xbf", [128, KD, NT], BF16)
        self.R_xres = [[Res("xres%d_%d" % (c, s)) for s in range(NSP)] for c in range(KD)]
        self.R_xbf = [[Res("xbf%d_%d" % (c, s)) for s in range(NSP)] for c in range(KD)]
        self.cbf = self.sb("cbf_sb", [128, nbf], BF16)
        self.cf32 = self.sb("cf32_sb", [128, NF32C], F32)
        self.lnp = self.sb("lnp_sb", [128, L * 3 * 2 * 8], F32)
        self.gb = self.sb("gb_sb", [128, L * 24], F32)
        self.sk = self.sb("sk_sb", [128, L * 8], F32)
        self.esk = self.sb("esk_sb", [128, 8], F32)
        self.R_esk = Res()
        self.R_const = Res("const")
        self.wsmall = Ring([(self.sb("wsm%d" % i, [128, 2048], BF16), Res()) for i in range(6)])
        self.wbig = Ring([(self.XRb[:, 16384 + i * 8192:16384 + (i + 1) * 8192], Res()) for i in range(2)])
        self.wmid = Ring([(self.XRb[:, 16384 + i * 4096:16384 + (i + 1) * 4096], Res()) for i in range(4)])
        self.psum = Ring([(self.st.enter_context(nc.psum_tensor("ps%d" % i, [128, 512], F32)), Res("ps%d" % i, excl=True))
                          for i in range(8)])
        self.fwork = self.sb("fwork", [128, 8192], BF16)
        self.hT = self.fwork[:].rearrange("p (c t) -> p c t", c=4)
        self.R_hT = [[Res() for s in range(NSP)] for c in range(4)]
        self.zbf = self.fwork[:, 0:4096].rearrange("p (c t) -> p c t", c=KD)
        self.zsq = self.fwork[:, 4096:8192].rearrange("p (c t) -> p c t", c=KD)
        self.R_zbf = Res("zbf")
        self.R_zsq = Res("zsq")
        self.kq = Ring([(self.fwork[:, i * 2048:(i + 1) * 2048], [Res() for s in range(NSP)]) for i in range(2)])
        self.kq2 = Ring([(self.fwork[:, 4096 + i * 2048:4096 + (i + 1) * 2048], [Res() for s in range(NSP)])
                         for i in range(2)])
        self.kq4 = Ring(self.kq.items + self.kq2.items)
        self.ysp = Ring([(self.fwork[:, i * 4096:(i + 1) * 4096], Res()) for i in range(2)])
        self.kx = self.XRb[:, 0:4096]
        self.R_kx = Res()
        self.vx = self.XRb[:, 4096:10240]
        self.R_vx = Res()
        self.R_vCt = [Res() for t in range(16)]
        self.kvr = Ring([((self.kx, self.R_kx), (self.vx, self.R_vx)),
                         ((self.XRb[:, 10240:14336], Res()), (self.XRb[:, 14336:20480], Res()))])
        self.acc = self.XR[:, 10240:14336].rearrange("p (n t) -> p n t", n=2)
        self.R_accf = [[Res() for rho in range(16)] for b in range(16)]
        self.R_acc = [[self.R_accf[b][rho] for b in range(4 * s, 4 * s + 4) for rho in range(16)] for s in range(NSP)]
        self.merged = self.XRb[:, 0:16384].rearrange("p (c t) -> p c t", c=KD)
        self.R_merged = [[Res() for s in range(NSP)] for c in range(KD)]
        self.tmpf = Ring([(self.sb("tmpf%d" % i, [128, 512], F32), Res()) for i in range(8)])
        self.tmpb = Ring([(self.sb("tmpb%d" % i, [128, 512], BF16), Res()) for i in range(8)])
        self.tmpb32 = Ring([(self.sb("lnt%d" % i, [128, SP], F32), Res()) for i in range(2)])
        self.R_lnmean = Res("lnmean")
        self.R_lnrstd = Res("lnrstd")
        self.cosT = self.sb("cosT", [128, NT], F32)
        self.sinT = self.sb("sinT", [128, NT], F32)
        self.R_rope = [Res() for s in range(NSP)]
        self.posi = self.sb("posi", [128, SP], I32)
        self.R_posi = Res()
        self.Sf = self.sb("Sf", [128, 128], F32)
        self.Sb = self.sb("Sb", [128, 128], BF16)
        self.R_Sf = Res()
        self.R_Sb = Res()

    def fwork_res(self, exclude=()):
        out = []
        for row in self.R_hT:
            out += row
        out += [self.R_zbf, self.R_zsq]
        for ring in (self.kq, self.kq2):
            for (_, rl) in ring.items:
                out += rl
        for (_, r) in self.ysp.items:
            out.append(r)
        ex = set(id(x) for x in exclude)
        return [r for r in out if id(r) not in ex]

    def arena_res(self):
        out = [r for (_, r) in self.wbig.items] + [r for (_, r) in self.wmid.items]
        for (kb, vb) in self.kvr.items:
            out += [kb[1], vb[1]]
        out += self.R_vCt
        for row in self.R_accf:
            out += row
        for row in self.R_merged:
            out += row
        return out

    def cb(self, name):
        o = 0
        for n in BF_NAMES:
            if n == name:
                return self.cbf[:, o:o + BF_W[n]]
            o += BF_W[n]
        raise KeyError(name)

    def cf(self, name):
        o, w = F32_OFFS[name]
        return self.cf32[:, o:o + w]

    def _program(self):
        S = self.S
        for dst, src in ((self.cbf, self.d_cbf), (self.cf32, self.d_cf32), (self.lnp, self.d_lnp),
                         (self.gb, self.d_gb), (self.sk, self.d_sk)):
            S.D("sp", dst[:], src, writes=[self.R_const])
        xv = self.d_xT.rearrange("(c p) t -> p c t", p=128)
        for c in range(KD):
            S.D("sp", self.xres[:, c, :], xv[:, c, :], writes=self.R_xres[c])
            for s in range(NSP):
                sl = slice(s * SP, (s + 1) * SP)
                S.I("act", "copy", [self.R_xres[c][s]], [self.R_xbf[c][s]], out=self.xbf[:, c, sl], in_=self.xres[:, c, sl])
        self.rope_tables()
        for l in range(DEPTH):
            self.ffn(l, self.d_f1u, self.d_f1d)
            if self.dbg == "ffn1":
                return self.store_x()
            self.layernorm(l, 0)
            if self.dbg == "ln1":
                return self.store_x()
            for s in range(NSP):
                sl = slice(s * SP, (s + 1) * SP)
                S.D("sp", self.xsp[:, :, sl], self.xres[:, :, sl], reads=[self.R_xres[c][s] for c in range(KD)],
                    writes=[self.R_xsp[s]])
            S.inherit(self.arena_res(), [r for row in self.R_xres for r in row])
            self.m1(l)
            self.exchange(l)
            S.barrier_engines()
            self.m2a(l)
            S.barrier()
            self.m2b(l)
            self.layernorm(l, 1)
            if self.dbg == "ln2":
                return self.store_x()
            self.ffn(l, self.d_f2u, self.d_f2d)
            self.layernorm(l, 2)
        self.store_x()

    def store_x(self):
        ov = self.d_out.rearrange("(c p) t -> p c t", p=128)
        for c in range(KD):
            self.S.D("sp", ov[:, c, :], self.xres[:, c, :], reads=self.R_xres[c], is_output=True)

    def load_w(self, dram_ap, ncols, nk, big=False, mid=False):
        if mid:
            wt, wr = self.wmid.next()
            assert nk * ncols <= 4096
            base = wt
        elif big:
            wt, wr = self.wbig.next()
            assert nk * ncols <= 8192
            base = wt
        else:
            wt, wr = self.wsmall.next()
            assert nk * ncols <= 2048
            base = wt[:]
        view = base[:, 0:nk * ncols].rearrange("p (k f) -> p k f", k=nk)
        src = dram_ap.rearrange("(k p) f -> p k f", p=128)
        self.S.D("pool", view, src, writes=[wr])
        return view, wr

    def rope_tables(self):
        S = self.S
        RC = self.R_const
        invf = self.cf("invf")
        two_pi = float(2.0 * np.pi)
        for s in range(NSP):
            sl = slice(s * SP, (s + 1) * SP)
            S.D("sp", self.posi[:], self.d_pos[:, sl].partition_broadcast(128), writes=[self.R_posi])
            t0, r0 = self.tmpf.next()
            t1, r1 = self.tmpf.next()
            t2, r2 = self.tmpf.next()
            S.I("dve", "tensor_copy", [self.R_posi], [r0], out=t0[:], in_=self.posi[:])
            S.I("dve", "tensor_scalar", [r0, RC], [r0], out=t0[:], in0=t0[:], scalar1=invf, scalar2=None, op0=ALU.mult)
            S.I("dve", "tensor_copy", [r0], [self.R_posi], out=self.posi[:], in_=t0[:])
            S.I("dve", "tensor_copy", [self.R_posi], [r1], out=t1[:], in_=self.posi[:])
            S.I("dve", "tensor_tensor", [r0, r1], [r0], out=t0[:], in0=t0[:], in1=t1[:], op=ALU.subtract)
            for (shift, dst) in ((0.0, self.sinT), (0.25, self.cosT)):
                S.I("dve", "tensor_scalar", [r0], [r2], out=t2[:], in0=t0[:], scalar1=float(shift), scalar2=None, op0=ALU.add)
                S.I("dve", "tensor_scalar", [r2], [r1], out=t1[:], in0=t2[:], scalar1=0.5, scalar2=-1.0,
                    op0=ALU.is_gt, op1=ALU.mult)
                S.I("dve", "tensor_tensor", [r2, r1], [r2], out=t2[:], in0=t2[:], in1=t1[:], op=ALU.add)
                S.I("dve", "tensor_scalar", [r2], [r1], out=t1[:], in0=t2[:], scalar1=-0.5, scalar2=None, op0=ALU.is_lt)
                S.I("dve", "tensor_tensor", [r2, r1], [r2], out=t2[:], in0=t2[:], in1=t1[:], op=ALU.add)
                S.I("act", "activation", [r2], [self.R_rope[s]], out=dst[:, sl], in_=t2[:], func=AF.Sin, scale=two_pi)

    def ffn(self, l, d_up, d_down):
        S = self.S
        cffn = 0.5 / ALPHA
        FG = 4
        groups = [(g * FG, min(FG, NFC - g * FG)) for g in range((NFC + FG - 1) // FG)]
        for (f0, nf) in groups:
            halves = [f0 + 2 * h for h in range(nf // 2)]
            ups = []
            for h0 in halves:
                wa, ra = self.load_w(d_up[l, :, h0 * 128:(h0 + 2) * 128], 256, KD)
                wb, rb = self.load_w(d_up[l, :, DFF + h0 * 128:DFF + (h0 + 2) * 128], 256, KD)
                ups.append((wa, ra, wb, rb))
            downs = [self.load_w(d_down[l, h0 * 128:(h0 + 2) * 128, :], D, 2) for h0 in halves]
            for ci in range(nf):
                wa, ra, wb, rb = ups[ci // 2]
                cj = ci % 2
                for s in range(NSP):
                    sl = slice(s * SP, (s + 1) * SP)
                    pa, rpa = self.psum.next()
                    pb, rpb = self.psum.next()
                    for k in range(KD):
                        S.I("pe", "matmul", [ra, self.R_xbf[k][s]], [rpa], pa[:], lhsT=wa[:, k, cj * 128:(cj + 1) * 128],
                            rhs=self.xbf[:, k, sl], start=(k == 0), stop=(k == KD - 1), _noinc=(k != KD - 1))
                    for k in range(KD):
                        S.I("pe", "matmul", [rb, self.R_xbf[k][s]], [rpb], pb[:], lhsT=wb[:, k, cj * 128:(cj + 1) * 128],
                            rhs=self.xbf[:, k, sl], start=(k == 0), stop=(k == KD - 1), _noinc=(k != KD - 1))
                    tf, rtf = self.tmpf.next()
                    S.I("act", "activation", [rpa], [rtf], out=tf[:], in_=pa[:], func=AF.Silu)
                    S.I("dve", "tensor_tensor", [rpb, rtf], [self.R_hT[ci][s]], out=self.hT[:, ci, sl], in0=pb[:], in1=tf[:],
                        op=ALU.mult)
            for dc in range(KD):
                for s in range(NSP):
                    sl = slice(s * SP, (s + 1) * SP)
                    py, rpy = self.psum.next()
                    for ci in range(nf):
                        wd, rd = downs[ci // 2]
                        S.I("pe", "matmul", [rd, self.R_hT[ci][s]], [rpy], py[:], lhsT=wd[:, ci % 2, dc * 128:(dc + 1) * 128],
                            rhs=self.hT[:, ci, sl], start=(ci == 0), stop=(ci == nf - 1), _noinc=(ci != nf - 1))
                    S.I("dve", "scalar_tensor_tensor", [rpy, self.R_xres[dc][s]], [self.R_xres[dc][s]],
                        out=self.xres[:, dc, sl], in0=py[:], scalar=cffn, in1=self.xres[:, dc, sl],
                        op0=ALU.mult, op1=ALU.add)
        hts = [r for row in self.R_hT for r in row]
        S.inherit(self.fwork_res(exclude=hts), hts)

    def layernorm(self, l, i):
        S = self.S
        ones = self.cb("ones_ln")
        eps = self.cf("eps_ln")
        base = ((l * 3 + i) * 2) * 8
        RC = self.R_const

        def stage_a(s):
            sl = slice(s * SP, (s + 1) * SP)
            rx = [self.R_xres[c][s] for c in range(KD)]
            S.I("act", "copy", rx, [self.R_zbf], out=self.zbf, in_=self.xres[:, :, sl])
            S.I("act", "activation", rx, [self.R_zsq], out=self.zsq, in_=self.xres[:, :, sl], func=AF.Square)
            pm, rpm = self.psum.next()
            pq, rpq = self.psum.next()
            for k in range(KD):
                S.I("pe", "matmul", [self.R_zbf, RC], [rpm], pm[:], lhsT=ones, rhs=self.zbf[:, k, :],
                    start=(k == 0), stop=(k == KD - 1), _noinc=(k != KD - 1))
            for k in range(KD):
                S.I("pe", "matmul", [self.R_zsq, RC], [rpq], pq[:], lhsT=ones, rhs=self.zsq[:, k, :],
                    start=(k == 0), stop=(k == KD - 1), _noinc=(k != KD - 1))
            mean, rmean = self.tmpf.next()
            rstd, rrstd = self.tmpf.next()
            S.I("act", "copy", [rpm], [rmean], out=mean[:], in_=pm[:])
            S.I("dve", "tensor_tensor", [rmean], [rrstd], out=rstd[:], in0=mean[:], in1=mean[:], op=ALU.mult)
            S.I("dve", "tensor_tensor", [rpq, rrstd], [rrstd], out=rstd[:], in0=pq[:], in1=rstd[:], op=ALU.subtract)
            S.I("act", "activation", [rrstd, RC], [rrstd], out=rstd[:], in_=rstd[:], func=AF.Ln, bias=eps)
            S.I("act", "activation", [rrstd], [rrstd], out=rstd[:], in_=rstd[:], func=AF.Exp, scale=-0.5)
            return (s, sl, mean, rmean, rstd, rrstd)

        def stage_b(st):
            s, sl, mean, rmean, rstd, rrstd = st
            for c in range(KD):
                t2, rt2 = self.tmpb32.next()
                rxc = self.R_xres[c][s]
                S.I("dve", "tensor_tensor", [rxc, rmean], [rt2], out=t2[:], in0=self.xres[:, c, sl], in1=mean[:], op=ALU.subtract)
                S.I("dve", "tensor_tensor", [rt2, rrstd], [rt2], out=t2[:], in0=t2[:], in1=rstd[:], op=ALU.mult)
                S.I("act", "activation", [rt2, RC], [rxc], out=self.xres[:, c, sl], in_=t2[:], func=AF.Identity,
                    scale=self.lnp[:, base + c:base + c + 1], bias=self.lnp[:, base + 8 + c:base + 8 + c + 1])
                S.I("act", "copy", [rxc], [self.R_xbf[c][s]], out=self.xbf[:, c, sl], in_=self.xres[:, c, sl])
        prev = stage_a(0)
        for s in range(1, NSP):
            cur = stage_a(s)
            stage_b(prev)
            prev = cur
        stage_b(prev)
        S.inherit(self.fwork_res(exclude=[self.R_zbf, self.R_zsq]), [self.R_zbf, self.R_zsq])

    def proj_rope(self, l, col0, dst, rdst, dup=False, perm_d=1):
        S = self.S
        wt, wr = self.wsmall.next()
        w = wt[:, 0:1024].rearrange("p (k f) -> p k f", k=KD)
        if dup:
            src = self.d_w_in[l, :, col0:col0 + 64].rearrange("(k p) f -> p k f", p=128)
            S.D("pool", w[:, :, 0:64], src, writes=[wr])
            S.D("pool", w[:, :, 64:128], src, writes=[wr])
        else:
            src = self.d_w_in[l, :, col0:col0 + 128].rearrange("(k p) f -> p k f", p=128)
            S.D("pool", w, src, writes=[wr])
        rot = self.cb("rot")

        def stage1(s):
            sl = slice(s * SP, (s + 1) * SP)
            px, rpx = self.psum.next()
            for k in range(KD):
                S.I("pe", "matmul", [wr, self.R_xbf[k][s]], [rpx], px[:], lhsT=w[:, k, :], rhs=self.xbf[:, k, sl],
                    start=(k == 0), stop=(k == KD - 1), _noinc=(k != KD - 1))
            xb, rxb = self.tmpb.next()
            S.I("act", "copy", [rpx], [rxb], out=xb[:], in_=px[:])
            return (s, sl, px, rpx, xb, rxb)

        def stage2(st):
            s, sl, px, rpx, xb, rxb = st
            pr, rpr = self.psum.next()
            S.I("pe", "matmul", [rxb, self.R_const], [rpr], pr[:], lhsT=rot, rhs=xb[:], start=True, stop=True)
            t1, r1 = self.tmpf.next()
            t2, r2 = self.tmpf.next()
            S.I("dve", "tensor_tensor", [rpx, self.R_rope[s]], [r1], out=t1[:], in0=px[:], in1=self.cosT[:, sl], op=ALU.mult)
            S.I("dve", "tensor_tensor", [rpr, self.R_rope[s]], [r2], out=t2[:], in0=pr[:], in1=self.sinT[:, sl], op=ALU.mult)
            if perm_d == 1 or not DESTRIDE:
                S.I("dve", "tensor_tensor", [r1, r2], [rdst[s]], out=dst[:, sl], in0=t1[:], in1=t2[:], op=ALU.add)
            else:
                n_i = SP // perm_d
                o3 = dst.rearrange("p (r i) -> p r i", r=perm_d)[:, :, s * n_i:(s + 1) * n_i].rearrange("p r i -> p i r")
                S.I("dve", "tensor_tensor", [r1, r2], [rdst[s]], out=o3, in0=t1[:].rearrange("p (i r) -> p i r", r=perm_d),
                    in1=t2[:].rearrange("p (i r) -> p i r", r=perm_d), op=ALU.add)
        prev = stage1(0)
        for s in range(1, NSP):
            cur = stage1(s)
            stage2(prev)
            prev = cur
        stage2(prev)

    def proj_tok(self, l, col0, ncols, dram_dst, row0, rdst, dcol0, halo=None, keep=None):
        S = self.S
        mid = ncols * KD > 2048
        w, wr = self.load_w(self.d_w_in[l, :, col0:col0 + ncols], ncols, KD, mid=mid)
        for t in range(NT // 128):
            s = t // 4
            ts = slice(t * 128, (t + 1) * 128)
            pv, rpv = self.psum.next()
            for k in range(KD):
                S.I("pe", "matmul", [wr, self.R_xbf[k][s]], [rpv], pv[:, 0:ncols], lhsT=self.xbf[:, k, ts], rhs=w[:, k, :],
                    start=(k == 0), stop=(k == KD - 1), _noinc=(k != KD - 1))
            if keep is not None:
                kt, rkl = keep
                rk = rkl[t]
                S.I("act", "copy", [rpv], [rk], out=kt[:, t, :], in_=pv[:, 0:ncols])
                src = kt[:, t, :]
                rsrc = rk
            else:
                vt, rvt = self.tmpb.next()
                S.I("act", "copy", [rpv], [rvt], out=vt[:, 0:ncols], in_=pv[:, 0:ncols])
                src = vt[:, 0:ncols]
                rsrc = rvt
            S.D("sp", dram_dst[row0 + t * 128:row0 + (t + 1) * 128, dcol0:dcol0 + ncols], src, reads=[rsrc], writes=[rdst])
            if halo is not None:
                ho, g = halo
                nrows = HLB if g == "b" else HL[g]
                if t * 128 >= NT - nrows:
                    row = t * 128 - (NT - nrows)
                    if g == "b":
                        hd, hr = ho["vb"], row
                    else:
                        ti, hr = vloc(g, row)
                        hd = ho["va"][ti]
                    S.D("sp", hd[hr:hr + 128, dcol0:dcol0 + ncols], src, reads=[rsrc], writes=[self.R_hout])

    def m1(self, l):
        S = self.S
        ho = self.hout.get(l)
        cut = self.dbg or ""
        if cut == "m1_0":
            return
        for g in range(3):
            oq, ok, ov = OFF_A[g]
            for j in range(6):
                if cut == "m1_1" and (g, j) != (0, 0):
                    continue
                kt, rk = self.kq4.next()
                dl = DIL[g] if DESTRIDE else 1
                hseg, Lseg = HL[g] // dl, NT // dl
                self.proj_rope(l, ok + 128 * j, kt, rk, perm_d=dl)
                kt3 = kt.rearrange("p (r i) -> p r i", r=dl)
                kd3 = self.kTd_A[g][j].rearrange("p (r e) -> p r e", r=dl)
                S.D("sp", kd3[:, :, hseg:hseg + Lseg], kt3, reads=rk, writes=[self.R_kTdA[g][j]])
                if ho is not None:
                    ti, c0 = kloc(g, j)
                    S.D("sp", ho["k"][ti][:, c0:c0 + HL[g]].rearrange("p (r e) -> p r e", r=dl), kt3[:, :, Lseg - hseg:Lseg],
                        reads=rk, writes=[self.R_hout])
            hal = (ho, g) if ho is not None else None
            if cut == "m1_1":
                return
            if cut == "m1_2" and g > 0:
                return
            self.proj_tok(l, ov, 512, self.Vd_A[g], HL[g], self.R_VdA[g], 0, halo=hal)
            self.proj_tok(l, ov + 512, 256, self.Vd_A[g], HL[g], self.R_VdA[g], 512, halo=hal)
        for kv in range(2):
            kt, rk = self.kq4.next()
            self.proj_rope(l, OFF_BK + 64 * kv, kt, rk, dup=True)
            S.D("sp", self.kTd_B[kv, :, HLB:HLB + NT], kt, reads=rk, writes=[self.R_kTdB[kv]])
            if ho is not None:
                ti, c0 = kloc_b(kv)
                S.D("sp", ho["k"][ti][:, c0:c0 + HLB], kt[:, NT - HLB:NT], reads=rk, writes=[self.R_hout])
        hal = (ho, "b") if ho is not None else None
        self.proj_tok(l, OFF_BV, 128, self.Vd_B, HLB, self.R_VdB, 0, halo=hal)
        if cut == "m1_3":
            return
        vC = self.vx[:, 0:4096].rearrange("p (n c) -> p n c", n=16)
        for j in range(4):
            kt, rk = self.kq4.next()
            self.proj_rope(l, OFF_CK + 128 * j, kt, rk)
            S.D("sp", self.kTd_C[j], kt, reads=rk, writes=[self.R_kTdC[j]])
            self.proj_tok(l, OFF_CV + 256 * j, 256, self.Vd_C, 0, self.R_VdC[j], 256 * j, keep=(vC, self.R_vCt))
            if ho is not None:
                self.retention_pair(l, j, kt, rk, vC, None, None, None, rv=self.R_vCt)
                S.D("sp", ho["s"][j], self.Sf[:], reads=[self.R_Sf], writes=[self.R_hout])

    def exchange(self, l):
        S = self.S
        groups = [[0, 1], [2, 3], [4, 5], [6, 7]]
        pairs = self.cc_pairs[l]
        self.R_hin_l = [Res() for _ in pairs]
        order = [len(pairs) - 1] + list(range(len(pairs) - 1))
        for i in order:
            src, dst = pairs[i]
            S.coll(lambda e, src=src, dst=dst: e.collective_compute(
                "AllGather", ALU.bypass, replica_groups=groups, ins=[src.opt()], outs=[dst.opt()]),
                reads=[self.R_hout], writes=[self.R_hin_l[i]])
        self.R_hin = self.R_hin_l[len(pairs) - 1]

    def exchange_copy(self, l):
        S = self.S
        hi = self.hin[l]
        nk = len(HK_T)
        for g in range(3):
            for j in range(6):
                ti, c0 = kloc(g, j)
                dl = DIL[g] if DESTRIDE else 1
                S.D("sp", self.kTd_A[g][j].rearrange("p (r e) -> p r e", r=dl)[:, :, 0:HL[g] // dl],
                    hi["k"][ti][:, c0:c0 + HL[g]].rearrange("p (r e) -> p r e", r=dl), reads=[self.R_hin_l[ti]],
                    writes=[self.R_kTdA[g][j]])
            row = 0
            while row < HL[g]:
                ti, hr = vloc(g, row)
                n = min(HL[g] - row, HV_T[ti][0] - hr)
                S.D("sp", self.Vd_A[g][row:row + n, :], hi["va"][ti][hr:hr + n, :], reads=[self.R_hin_l[nk + ti]],
                    writes=[self.R_VdA[g]])
                row += n
        for kv in range(2):
            ti, c0 = kloc_b(kv)
            S.D("sp", self.kTd_B[kv, :, 0:HLB], hi["k"][ti][:, c0:c0 + HLB], reads=[self.R_hin_l[ti]], writes=[self.R_kTdB[kv]])
        S.D("sp", self.Vd_B[0:HLB, :], hi["vb"], reads=[self.R_hin_l[nk + len(HV_T)]], writes=[self.R_VdB])

    def unit_qk(self, job, r, blk):
        S = self.S
        d = job["d"]
        qT, rq, kx, rkx = job["qT"], job["rq"], job["kx"], job["rkx"]
        q0 = r + d * 128 * blk
        qsl = slice(q0, q0 + 127 * d + 1, d)
        spans = list(range(q0 // SP, (q0 + 127 * d) // SP + 1))
        rqs = [rq[s] for s in spans]
        mmask = job["mmask0"] if blk == 0 else job["mmask"]
        pt, rpt = self.tmpb.next()
        kx3 = kx[:, 0:job["klen"]].rearrange("p (r e) -> p r e", r=d)
        qT3 = qT.rearrange("p (r i) -> p r i", r=d)
        for hh in range(2):
            ps, rps = self.psum.next()
            hs = slice(64 * hh, 64 * hh + 64)
            for tt in range(2):
                b1 = blk + tt
                if DESTRIDE or d == 1:
                    kap = kx3[hs, r, 128 * b1:128 * b1 + 128]
                    qap = qT3[hs, r, 128 * blk:128 * blk + 128]
                else:
                    k0 = r + d * 128 * b1
                    kap = kx[hs, k0:k0 + 127 * d + 1:d]
                    qap = qT[hs, qsl]
                S.I("pe", "matmul", [rkx] + rqs, [rps], ps[:, tt * 128:tt * 128 + 128], lhsT=kap, rhs=qap, start=True, stop=True,
                    _noinc=(tt == 0))
            S.I("act", "activation", [rps], [rpt], out=pt[:, 256 * hh:256 * hh + 256], in_=ps[:, 0:256], func=AF.Exp, scale=0.125)
        S.I("dve", "tensor_tensor", [rpt, self.R_const], [rpt], out=pt[:], in0=pt[:], in1=mmask, op=ALU.mult)
        return (job, r, blk, qsl, spans, pt, rpt)

    def unit_pv(self, st):
        S = self.S
        job, r, blk, qsl, spans, pt, rpt = st
        nb1 = job["nb1"]
        vt = job["vx"].rearrange("p (n c) -> p n c", c=192)
        rvx = job["rvx"]
        p2, rp2 = self.psum.next()
        for hh in range(2):
            for tt in range(2):
                c0 = (2 * hh + tt) * 128
                ti = r * nb1 + blk + tt
                S.I("pe", "matmul", [rvx, rpt], [rp2], p2[:, 128 * hh:128 * hh + 128], lhsT=vt[:, ti, 64 * hh:64 * hh + 128],
                    rhs=pt[:, c0:c0 + 128], start=(tt == 0), stop=(tt == 1), _noinc=not (hh == 1 and tt == 1))
        d = job["d"]
        q0 = r + d * 128 * blk
        blocks = range(q0 // 128, (q0 + 127 * d) // 128 + 1)
        rhos = [(r + d * m) % 16 for m in range(max(1, 16 // d))] if d > 1 else list(range(16))
        racc = [self.R_accf[b][rho] for b in blocks for rho in rhos]
        dst = self.acc[:, :, qsl]
        src = p2[:, 0:256].rearrange("p (n t) -> p n t", n=2)
        if job["first"]:
            S.I("act", "copy", [rp2], racc, out=dst, in_=src)
        else:
            S.I("dve", "tensor_tensor", [rp2] + racc, racc, out=dst, in0=src, in1=dst, op=ALU.add)

    def job_prep(self, l, job):
        S = self.S
        qT, rq = self.kq.next()
        self.proj_rope(l, job["qcol"], qT, rq, perm_d=job["d"])
        (kx, rkx), (vx, rvx) = self.kvr.next()
        d, nb1 = job["d"], job["nb1"]
        S.D("sp", kx[:, 0:job["klen"]], job["kT_src"], reads=[job["rk_src"]], writes=[rkx])
        vt = vx.rearrange("p (n c) -> p n c", c=192)
        Vd, vcol0, shared = job["Vd"], job["vcol0"], job["shared_kv"]
        vt4 = vt[:, 0:d * nb1, :].rearrange("p (r b) c -> p b r c", r=d)
        for hh in range(2):
            vc = vcol0 if shared else vcol0 + 64 * hh
            if d > nb1:
                src4 = Vd[0:d * 128 * nb1, vc:vc + 64].rearrange("(b i r) c -> i b r c", i=128, r=d)
                for b in range(nb1):
                    S.D("sp", vt4[:, b, :, 128 * hh:128 * hh + 64], src4[:, b, :, :], reads=[job["rV"]], writes=[rvx])
            else:
                for r in range(d):
                    rows = slice(r, r + d * (128 * nb1 - 1) + 1, d)
                    src = Vd[rows, vc:vc + 64].rearrange("(b i) c -> i b c", i=128)
                    S.D("sp", vt[:, r * nb1:(r + 1) * nb1, 128 * hh:128 * hh + 64], src, reads=[job["rV"]], writes=[rvx])
        job.update(qT=qT, rq=rq, kx=kx, rkx=rkx, vx=vx, rvx=rvx)

    def attention_jobs(self, l):
        maskA, maskA0 = self.cb("maskA"), self.cb("maskA0")
        maskB, maskB0 = self.cb("maskB"), self.cb("maskB0")
        jobs = []
        for j in range(8):
            kv = j // 4
            jobs.append(dict(d=1, nblk=16, nb1=17, qcol=OFF_BQ + 128 * j, kT_src=self.kTd_B[kv], rk_src=self.R_kTdB[kv],
                             klen=HLB + NT, Vd=self.Vd_B, rV=self.R_VdB, vcol0=64 * kv, shared_kv=True, mask=maskB,
                             mask0=maskB0, mmask=self.cb("mmaskB"), mmask0=self.cb("mmaskB0"), first=True, finish=(6 + j, self.esk[:, j:j + 1])))
        for j in range(6):
            for g in range(3):
                d = DIL[g]
                jobs.append(dict(d=d, nblk=16 // d, nb1=16 // d + 1, qcol=OFF_A[g][0] + 128 * j, kT_src=self.kTd_A[g][j],
                                 rk_src=self.R_kTdA[g][j], klen=HL[g] + NT, Vd=self.Vd_A[g], rV=self.R_VdA[g], vcol0=128 * j,
                                 shared_kv=False, mask=maskA, mask0=maskA0, mmask=self.cb("mmaskA"), mmask0=self.cb("mmaskA0"),
                                 first=(g == 0),
                                 finish=((j, None) if g == 2 else None)))
        for (kb, vb) in self.kvr.items:
            vt = vb[0].rearrange("p (n c) -> p n c", c=192)
            self.S.I("pool", "memset", [], [vb[1]], vt[:, :, 64:128], 1.0)
        self.job_prep(l, jobs[0])
        DEPTH_Q = 2
        pend = []

        def pop_one():
            st, fin = pend.pop(0)
            self.unit_pv(st)
            if fin is not None:
                self.attn_finish(fin[0], extra_den=fin[1])
        for idx, job in enumerate(jobs):
            first_unit = True
            nun = job["d"] * job["nblk"]
            ui = 0
            for r in range(job["d"]):
                for blk in range(job["nblk"]):
                    cur = self.unit_qk(job, r, blk)
                    ui += 1
                    pend.append((cur, job["finish"] if ui == nun else None))
                    if len(pend) > DEPTH_Q:
                        pop_one()
                    if first_unit and all(p[0][0] is job for p in pend):
                        first_unit = False
                        if idx + 1 < len(jobs):
                            self.job_prep(l, jobs[idx + 1])
            assert not first_unit
        while pend:
            pop_one()

    def attn_finish(self, ychunk, extra_den=None):
        S = self.S
        RC = self.R_const
        shdn, shup = self.cf("shdn"), self.cf("shup")
        yt, ry = self.kq2.next()
        for s in range(NSP):
            sl = slice(s * SP, (s + 1) * SP)
            pd, rpd = self.psum.next()
            S.I("pe", "matmul", self.R_acc[s] + [RC], [rpd], pd[:], lhsT=shdn, rhs=self.acc[:, 0, sl], start=True, stop=False)
            S.I("pe", "matmul", self.R_acc[s] + [RC], [rpd], pd[:], lhsT=shup, rhs=self.acc[:, 1, sl], start=False, stop=True)
            tf, rtf = self.tmpf.next()
            if extra_den is not None:
                S.I("act", "activation", [rpd, self.R_esk], [rtf], out=tf[:], in_=pd[:], func=AF.Ln, bias=extra_den)
            else:
                S.I("act", "activation", [rpd], [rtf], out=tf[:], in_=pd[:], func=AF.Ln)
            S.I("act", "activation", [rtf], [rtf], out=tf[:], in_=tf[:], func=AF.Exp, scale=-1.0)
            for hh in range(2):
                hs = slice(64 * hh, 64 * hh + 64)
                S.I("dve", "tensor_tensor", self.R_acc[s] + [rtf], [ry[s]], out=yt[hs, sl], in0=self.acc[hs, hh, sl], in1=tf[hs, :],
                    op=ALU.mult)
        S.D("sp", self.yT_d[ychunk], yt, reads=ry, writes=[self.R_yTd[ychunk]])

    def load_kv(self, kT_src, rk_src, klen, Vd, rV, d, vcol0, vcols, nb1):
        S = self.S
        S.D("sp", self.kx[:, 0:klen], kT_src, reads=[rk_src], writes=[self.R_kx])
        vt = self.vx.rearrange("p (n c) -> p n c", c=128)
        for r in range(d):
            src = Vd[r:r + d * (128 * nb1 - 1) + 1:d, vcol0:vcol0 + vcols].rearrange("(b i) c -> i b c", i=128)
            S.D("sp", vt[:, r * nb1:(r + 1) * nb1, 0:vcols], src, reads=[rV], writes=[self.R_vx])

    def m2a(self, l):
        S = self.S
        S.I("act", "activation", [self.R_const], [self.R_esk], out=self.esk[:], in_=self.sk[:, l * 8:(l + 1) * 8], func=AF.Exp)
        vC = self.vx[:, 0:4096].rearrange("p (n c) -> p n c", n=16)
        for j in range(4):
            qT, rq = self.kq.next()
            self.proj_rope(l, OFF_CQ + 128 * j, qT, rq)
            S.D("sp", self.kx[:, 0:NT], self.kTd_C[j], reads=[self.R_kTdC[j]], writes=[self.R_kx])
            src = self.Vd_C[:, 256 * j:256 * j + 256].rearrange("(n i) c -> i n c", i=128)
            S.D("sp", vC, src, reads=[self.R_VdC[j]], writes=[self.R_vx])
            self.retention_pair(l, j, self.kx[:, 0:NT], [self.R_kx] * NSP, vC, qT, rq, self.hin[l]["s"][j])
            self.retention_finish(l, j)
        self.exchange_copy(l)
        self.attention_jobs(l)

    def retention_pair(self, l, j, kT, rk, vC, qT, rq, s0_dram, rv=None):
        S = self.S
        if rv is None:
            rv = [self.R_vx] * 16
        RC = self.R_const
        intra = self.cf("intra")[:, 256 * j:256 * j + 256]
        qdec = self.cf("qdec")[:, 128 * j:128 * j + 128]
        kdec = self.cf("kdec")[:, 128 * j:128 * j + 128]
        cdec = self.cf("cdec")[:, j:j + 1]
        ident = self.cb("ident")
        outputs = qT is not None
        if s0_dram is not None:
            S.D("sp", self.Sf[:], s0_dram, reads=[self.R_hin], writes=[self.R_Sf])
            S.I("dve", "tensor_scalar", [self.R_Sf, RC], [self.R_Sf], out=self.Sf[:], in0=self.Sf[:], scalar1=self.cf("hv"),
                scalar2=None, op0=ALU.mult)
        else:
            S.I("dve", "memset", [], [self.R_Sf], self.Sf[:], 0.0)
        S.I("act", "copy", [self.R_Sf], [self.R_Sb], out=self.Sb[:], in_=self.Sf[:])
        if outputs:
            qd, rqd = self.kq2.next()
            for s in range(NSP):
                sl = slice(s * SP, (s + 1) * SP)
                S.I("dve", "tensor_tensor", [rq[s], RC], [rqd[s]], out=qd[:, sl].rearrange("p (n t) -> p n t", t=128),
                    in0=qT[:, sl].rearrange("p (n t) -> p n t", t=128),
                    in1=qdec.unsqueeze(1).broadcast_to([128, 4, 128]), op=ALU.mult)
        def stage1(n):
            s = n // 4
            cs = slice(n * 128, (n + 1) * 128)
            pT, rpT = self.psum.next()
            pTb = pT[:].bitcast(BF16)
            S.I("pe", "transpose", [rk[s], RC], [rpT], pTb[:, 0:128], kT[:, cs], ident)
            kd, rkd = self.tmpb.next()
            S.I("dve", "tensor_tensor", [rpT, RC], [rkd], out=kd[:, 0:128], in0=pTb[:, 0:128], in1=kdec, op=ALU.mult)
            aT = raT = None
            if outputs:
                aT, raT = self.tmpb.next()
                for hh in range(2):
                    hs = slice(64 * hh, 64 * hh + 64)
                    pa, rpa = self.psum.next()
                    S.I("pe", "matmul", [rk[s], rq[s]], [rpa], pa[:, 0:128], lhsT=kT[hs, cs], rhs=qT[hs, cs], start=True, stop=True)
                    S.I("dve", "tensor_tensor", [rpa, RC], [raT], out=aT[:, 128 * hh:128 * hh + 128], in0=pa[:, 0:128],
                        in1=intra[:, 128 * hh:128 * hh + 128], op=ALU.mult)
            return (n, s, cs, kd, rkd, aT, raT)

        def stage2(st):
            n, s, cs, kd, rkd, aT, raT = st
            if outputs:
                for hh in range(2):
                    hs = slice(64 * hh, 64 * hh + 64)
                    po, rpo = self.psum.next()
                    S.I("pe", "matmul", [rv[n], raT], [rpo], po[:, 0:128],
                        lhsT=vC[:, n, 128 * hh:128 * hh + 128], rhs=aT[:, 128 * hh:128 * hh + 128], start=True, stop=False)
                    S.I("pe", "matmul", [self.R_Sb, rqd[s]], [rpo], po[:, 0:128],
                        lhsT=self.Sb[hs, :], rhs=qd[hs, cs], start=False, stop=True)
                    S.I("act", "copy", [rpo], self.R_acc[s], out=self.acc[:, hh, cs], in_=po[:, 0:128])
            pu, rpu = self.psum.next()
            for hh in range(2):
                hs = slice(64 * hh, 64 * hh + 64)
                S.I("pe", "matmul", [rkd, rv[n]], [rpu], pu[hs, 0:128], lhsT=kd[:, hs], rhs=vC[:, n, 128 * hh:128 * hh + 128],
                    start=True, stop=True)
            S.I("dve", "scalar_tensor_tensor", [self.R_Sf, rpu, RC], [self.R_Sf], out=self.Sf[:], in0=self.Sf[:], scalar=cdec,
                in1=pu[:, 0:128], op0=ALU.mult, op1=ALU.add)
            if outputs:
                S.I("act", "copy", [self.R_Sf], [self.R_Sb], out=self.Sb[:], in_=self.Sf[:])
        NCH = NT // 128
        cur = stage1(0)
        for n in range(NCH):
            nxt = stage1(n + 1) if n + 1 < NCH else None
            stage2(cur)
            cur = nxt

    def retention_finish(self, l, j):
        S = self.S
        RC = self.R_const
        ones = self.cb("ones_gn")
        eps = self.cf("eps_gn")

        def stage_a(hh, s, wg, rwg):
            sl = slice(s * SP, (s + 1) * SP)
            r_ap = self.acc[:, hh, sl]
            rb_, rrb = self.tmpb.next()
            rs_, rrs = self.tmpb.next()
            S.I("act", "copy", self.R_acc[s], [rrb], out=rb_[:], in_=r_ap)
            S.I("act", "activation", self.R_acc[s], [rrs], out=rs_[:], in_=r_ap, func=AF.Square)
            pm, rpm = self.psum.next()
            pq, rpq = self.psum.next()
            S.I("pe", "matmul", [rrb, RC], [rpm], pm[:], lhsT=ones, rhs=rb_[:], start=True, stop=True)
            S.I("pe", "matmul", [rrs, RC], [rpq], pq[:], lhsT=ones, rhs=rs_[:], start=True, stop=True)
            pg, rpg = self.psum.next()
            for k in range(KD):
                S.I("pe", "matmul", [rwg, self.R_xbf[k][s]], [rpg], pg[:], lhsT=wg[:, k, :], rhs=self.xbf[:, k, sl],
                    start=(k == 0), stop=(k == KD - 1), _noinc=(k != KD - 1))
            m_, rm = self.tmpf.next()
            v_, rv = self.tmpf.next()
            S.I("act", "copy", [rpm], [rm], out=m_[:], in_=pm[:])
            S.I("dve", "tensor_tensor", [rm], [rv], out=v_[:], in0=m_[:], in1=m_[:], op=ALU.mult)
            S.I("dve", "tensor_tensor", [rpq, rv], [rv], out=v_[:], in0=pq[:], in1=v_[:], op=ALU.subtract)
            return (hh, s, sl, r_ap, m_, rm, v_, rv, pg, rpg)

        def stage_b(st, yt, ry):
            hh, s, sl, r_ap, m_, rm, v_, rv, pg, rpg = st
            t_, rt = self.tmpf.next()
            g_, rg = self.tmpf.next()
            S.I("act", "activation", [rv, RC], [rv], out=v_[:], in_=v_[:], func=AF.Ln, bias=eps)
            S.I("act", "activation", [rv], [rv], out=v_[:], in_=v_[:], func=AF.Exp, scale=-0.5)
            S.I("act", "activation", [rpg], [rg], out=g_[:], in_=pg[:], func=AF.Silu)
            S.I("dve", "tensor_tensor", self.R_acc[s] + [rm], [rt], out=t_[:], in0=r_ap, in1=m_[:], op=ALU.subtract)
            S.I("dve", "tensor_tensor", [rt, rv], [rt], out=t_[:], in0=t_[:], in1=v_[:], op=ALU.mult)
            S.I("dve", "tensor_tensor", [rt, rg], [ry[s]], out=yt[:, sl], in0=t_[:], in1=g_[:], op=ALU.mult)
        for hh in range(2):
            hd = 2 * j + hh
            wg, rwg = self.load_w(self.d_w_in[l, :, OFF_CG + 128 * hd:OFF_CG + 128 * hd + 128], 128, KD)
            yt, ry = self.kq2.next()
            prev = stage_a(hh, 0, wg, rwg)
            for s in range(1, NSP):
                cur = stage_a(hh, s, wg, rwg)
                stage_b(prev, yt, ry)
                prev = cur
            stage_b(prev, yt, ry)
            S.D("sp", self.yT_d[14 + hd], yt, reads=ry, writes=[self.R_yTd[14 + hd]])

    def m2b(self, l):
        S = self.S
        RC = self.R_const
        branches = ((0, 6, 0, self.d_wpa), (1, 8, 6, self.d_wpb), (2, 8, 14, self.d_wpc))
        for (X, KX, yb, wp_d) in branches:
          for h in range(2):
            c0 = OFF_G + 1024 * X + 512 * h
            wg, rwg = self.load_w(self.d_w_in[l, :, c0:c0 + 512], 512, KD, mid=True)
            wp, rwp = self.load_w(wp_d[l, :, 512 * h:512 * h + 512], 512, KX, mid=True)
            for s in range(NSP):
                sl = slice(s * SP, (s + 1) * SP)
                yt, ry = self.ysp.next()
                ysp = yt[:, 0:KX * SP].rearrange("p (k t) -> p k t", k=KX)
                S.D("sp", ysp, self.yT_d[yb:yb + KX, :, sl].rearrange("k p t -> p k t"),
                    reads=[self.R_yTd[yb + k] for k in range(KX)], writes=[ry])
                for dcl in range(4):
                    dc = 4 * h + dcl
                    dsl = slice(dcl * 128, (dcl + 1) * 128)
                    pg, rpg = self.psum.next()
                    for k in range(KD):
                        S.I("pe", "matmul", [rwg, self.R_xbf[k][s]], [rpg], pg[:], lhsT=wg[:, k, dsl], rhs=self.xbf[:, k, sl],
                            start=(k == 0), stop=(k == KD - 1), _noinc=(k != KD - 1))
                    g_, rg = self.tmpf.next()
                    bi = l * 24 + 8 * X + dc
                    S.I("act", "activation", [rpg, RC], [rg], out=g_[:], in_=pg[:], func=AF.Sigmoid, bias=self.gb[:, bi:bi + 1])
                    pp, rpp = self.psum.next()
                    for k in range(KX):
                        S.I("pe", "matmul", [rwp, ry], [rpp], pp[:], lhsT=wp[:, k, dsl], rhs=ysp[:, k, :],
                            start=(k == 0), stop=(k == KX - 1), _noinc=(k != KX - 1))
                    rm = self.R_merged[dc][s]
                    if X == 0:
                        S.I("dve", "tensor_tensor", [rpp, rg], [rm], out=self.merged[:, dc, sl], in0=pp[:], in1=g_[:], op=ALU.mult)
                    else:
                        S.I("dve", "tensor_tensor", [rpp, rg], [rg], out=g_[:], in0=pp[:], in1=g_[:], op=ALU.mult)
                        S.I("dve", "tensor_tensor", [rg, rm], [rm], out=self.merged[:, dc, sl], in0=g_[:],
                            in1=self.merged[:, dc, sl], op=ALU.add)
        S.barrier()
        wo, rwo = self.load_w(self.d_wout[l], 1024, KD, big=True)
        for s in range(NSP):
            sl = slice(s * SP, (s + 1) * SP)
            rms = [self.R_merged[c][s] for c in range(KD)]
            S.I("act", "copy", rms, [self.R_zbf], out=self.zbf, in_=self.merged[:, :, sl])
            pos = []
            for dc in range(KD):
                po, rpo = self.psum.next()
                for k in range(KD):
                    S.I("pe", "matmul", [rwo, self.R_zbf], [rpo], po[:], lhsT=wo[:, k, dc * 128:(dc + 1) * 128],
                        rhs=self.zbf[:, k, :], start=(k == 0), stop=(k == KD - 1), _noinc=(k != KD - 1))
                t_, rt = self.tmpf.next()
                S.I("act", "copy", [rpo], [rt], out=t_[:], in_=po[:])
                S.D("sp", self.mixd[dc, :, sl], t_[:], reads=[rt], writes=[self.R_mixd[s]])
        S.barrier()
        for s in range(NSP):
            sl = slice(s * SP, (s + 1) * SP)
            for c in range(KD):
                S.D("sp", self.xres[:, c, sl], self.xsp[:, c, sl], reads=[self.R_xsp[s]], writes=[self.R_xres[c][s]])
                t_, rt = self.tmpf.next()
                S.D("sp", t_[:], self.mixd[c, :, sl], reads=[self.R_mixd[s]], writes=[rt])
                S.I("dve", "scalar_tensor_tensor", [rt, self.R_xres[c][s]], [self.R_xres[c][s]], out=self.xres[:, c, sl],
                    in0=t_[:], scalar=1.0 / ALPHA, in1=self.xres[:, c, sl], op0=ALU.mult, op1=ALU.add)
        S.barrier()


def _small_params(ln1_g, ln1_b, ln2_g, ln2_b, ln3_g, ln3_b, gate_bias, attn_sinks):
    L = DEPTH
    lnp = np.zeros((128, L, 3, 2, 8), np.float32)
    for l in range(L):
        for i, (g, b) in enumerate(((ln1_g, ln1_b), (ln2_g, ln2_b), (ln3_g, ln3_b))):
            lnp[:, l, i, 0, :] = np.asarray(g[l], np.float32).reshape(8, 128).T
            lnp[:, l, i, 1, :] = np.asarray(b[l], np.float32).reshape(8, 128).T
    gb = np.zeros((128, L, 24), np.float32)
    sk = np.zeros((128, L, 8), np.float32)
    for l in range(L):
        gb[:, l, :] = np.asarray(gate_bias[l], np.float32).reshape(24, 128).T
        s = np.asarray(attn_sinks[l], np.float32).reshape(8, 2)
        sk[:64, l, :] = s[:, 0][None, :]
        sk[64:, l, :] = s[:, 1][None, :]
    return lnp.reshape(128, -1), gb.reshape(128, -1), sk.reshape(128, -1)


def make_in_maps(inputs, cores):
    x = np.asarray(inputs["x"], np.float32)
    pos = np.asarray(inputs["positions"], np.int32)
    lnp, gb, sk = _small_params(inputs["ln1_g"], inputs["ln1_b"], inputs["ln2_g"], inputs["ln2_b"],
                                inputs["ln3_g"], inputs["ln3_b"], inputs["gate_bias"], inputs["attn_sinks"])
    shared = {k: np.ascontiguousarray(np.asarray(inputs[k], np.float32)) for k in
              ("w_in", "w_proj_a", "w_proj_b", "w_proj_c", "w_out", "ffn1_up", "ffn1_down", "ffn2_up", "ffn2_down")}
    maps = []
    for c in cores:
        b, h = c // 2, c % 2
        m = dict(shared)
        m["xT"] = np.ascontiguousarray(x[b, h * NT:(h + 1) * NT, :].T)
        m["pos"] = np.ascontiguousarray(pos[b, h * NT:(h + 1) * NT].reshape(1, NT))
        m["lnp"] = lnp
        m["gbias"] = gb
        m["sinks"] = sk
        m["cbf"] = _bf_pack(h == 1)
        m["cf32"] = _f32_consts(h == 1)[0]
        maps.append(m)
    return maps


_PROGS = {}


def _get_prog(stage):
    if stage not in _PROGS:
        _PROGS[stage] = Prog(stage).build()
    return _PROGS[stage]


def kernel(**inputs):
    cores = list(range(8))
    maps = make_in_maps(inputs, cores)
    nc = _get_prog("F")
    res = run_bass_kernel_spmd(nc, maps, core_ids=cores).results
    out = np.zeros((4, 2 * NT, D), np.float32)
    for c in cores:
        b, h = c // 2, c % 2
        out[b, h * NT:(h + 1) * NT, :] = np.asarray(res[c]["outT"], np.float32).T
    return out
```
